# Optimizing a Trainium2 kernel written in Bass

```python
import jax, jax.numpy as jnp
from jax import lax
import numpy as np

D_MODEL = 1024
BATCH = 4
SEQ = 4096
DEPTH = 4
DEC_BATCH = 16
DEC_SEQ = 2048
PAST_LEN = 128

N_META = 16
GRID_W = 64
HEAD_DIM = 64
N_Q_HEADS = 16
N_KV_HEADS = 4
Q_PER_KV = N_Q_HEADS // N_KV_HEADS
ATTN_WIDTH = N_Q_HEADS * HEAD_DIM
KV_WIDTH = N_KV_HEADS * HEAD_DIM
CONV_WIDTH = D_MODEL
CONV_K = 3
D_FF = 2816
Q_BLOCK = 128
ROPE_BASE = 10000.0
NORM_EPS = 1e-6
ROPE_HALF = HEAD_DIM // 2
IN_WIDTH = ATTN_WIDTH + 2 * KV_WIDTH + 3 * CONV_WIDTH + 2 * D_MODEL

kernel_name = "hybrid_gqa_shortconv_macaron_encoder"


def rmsnorm(x, g):
    xf = x.astype(jnp.float32)
    y = xf * lax.rsqrt(jnp.mean(xf * xf, axis=-1, keepdims=True) + NORM_EPS)
    return (y * g.astype(jnp.float32)).astype(x.dtype)


def swiglu(x, w_gate, w_up, w_down):
    return (jax.nn.silu(x @ w_gate) * (x @ w_up)) @ w_down


def axial_angles(n_tok):
    rows = n_tok // GRID_W
    row = jnp.repeat(jnp.arange(rows, dtype=jnp.float32), GRID_W)
    col = jnp.tile(jnp.arange(GRID_W, dtype=jnp.float32), rows)
    meta_row = jnp.full((N_META,), -1.0, dtype=jnp.float32)
    meta_col = jnp.arange(N_META, dtype=jnp.float32)
    row = jnp.concatenate([meta_row, row])
    col = jnp.concatenate([meta_col, col])
    inv_freq = ROPE_BASE ** (-jnp.arange(0, ROPE_HALF, 2, dtype=jnp.float32) / ROPE_HALF)
    ang_r = row[:, None] * inv_freq[None, :]
    ang_c = col[:, None] * inv_freq[None, :]
    return jnp.cos(ang_r), jnp.sin(ang_r), jnp.cos(ang_c), jnp.sin(ang_c)


def rope_rotate(x, cos, sin):
    x1, x2 = jnp.split(x, 2, axis=-1)
    c = cos[None, :, None, :]
    s = sin[None, :, None, :]
    return jnp.concatenate([x1 * c - x2 * s, x2 * c + x1 * s], axis=-1)


def axial_rope(x, angles):
    cos_r, sin_r, cos_c, sin_c = angles
    xf = x.astype(jnp.float32)
    out = jnp.concatenate([rope_rotate(xf[..., :ROPE_HALF], cos_r, sin_r),
                           rope_rotate(xf[..., ROPE_HALF:], cos_c, sin_c)], axis=-1)
    return out.astype(x.dtype)


def blocked_gqa(q, k, v):
    b, l = q.shape[0], q.shape[1]

    def attend(qb):
        s = jnp.einsum('bqkgd,bskd->bkgqs', qb, k).astype(jnp.float32)
        p = jax.nn.softmax(s, axis=-1).astype(v.dtype)
        return jnp.einsum('bkgqs,bskd->bqkgd', p, v)

    meta_out = attend(q[:, :N_META])
    real = q[:, N_META:]
    n_blk = real.shape[1] // Q_BLOCK
    blocks = real.reshape(b, n_blk, Q_BLOCK, N_KV_HEADS, Q_PER_KV, HEAD_DIM).transpose(1, 0, 2, 3, 4, 5)
    out = lax.map(attend, blocks)
    out = out.transpose(1, 0, 2, 3, 4, 5).reshape(b, n_blk * Q_BLOCK, N_KV_HEADS, Q_PER_KV, HEAD_DIM)
    return jnp.concatenate([meta_out, out], axis=1).reshape(b, l, ATTN_WIDTH)


def centred_dwconv(x, w, bias):
    xp = jnp.pad(x, ((0, 0), (1, 1), (0, 0)))
    return w[0] * xp[:, :-2] + w[1] * xp[:, 1:-1] + w[2] * xp[:, 2:] + bias


def token_mixer(u, angles, w_in, conv_w, conv_b, q_norm, k_norm, w_o_attn, w_o_conv, w_merge):
    b, l, _ = u.shape
    p = u @ w_in
    splits = np.cumsum([ATTN_WIDTH, KV_WIDTH, KV_WIDTH, CONV_WIDTH, CONV_WIDTH, CONV_WIDTH, D_MODEL])
    q, k, v, cb, cc, cx, g_attn, g_conv = jnp.split(p, splits, axis=-1)
    q = q.reshape(b, l, N_Q_HEADS, HEAD_DIM)
    k = k.reshape(b, l, N_KV_HEADS, HEAD_DIM)
    v = v.reshape(b, l, N_KV_HEADS, HEAD_DIM)
    q = axial_rope(rmsnorm(q, q_norm), angles) * (HEAD_DIM ** -0.5)
    k = axial_rope(rmsnorm(k, k_norm), angles)
    q = q.reshape(b, l, N_KV_HEADS, Q_PER_KV, HEAD_DIM)
    a = blocked_gqa(q, k, v) @ w_o_attn
    c = (cb * centred_dwconv(cc * cx, conv_w, conv_b)) @ w_o_conv
    merged = jax.nn.sigmoid(g_attn) * a + jax.nn.sigmoid(g_conv) * c
    return merged @ w_merge


def trunk(x, meta_tokens, ffn1_norm, ffn1_w_gate, ffn1_w_up, ffn1_w_down, mix_norm, w_in, conv_w,
          conv_b, q_norm, k_norm, w_o_attn, w_o_conv, w_merge, ffn2_norm, ffn2_w_gate, ffn2_w_up,
          ffn2_w_down, final_norm):
    b, n_tok, _ = x.shape
    meta = jnp.broadcast_to(meta_tokens.astype(x.dtype)[None], (b, N_META, D_MODEL))
    h = jnp.concatenate([meta, x], axis=1)
    angles = axial_angles(n_tok)
    for i in range(DEPTH):
        h = h + 0.5 * swiglu(rmsnorm(h, ffn1_norm[i]), ffn1_w_gate[i], ffn1_w_up[i], ffn1_w_down[i])
        h = h + token_mixer(rmsnorm(h, mix_norm[i]), angles, w_in[i], conv_w[i], conv_b[i], q_norm[i],
                            k_norm[i], w_o_attn[i], w_o_conv[i], w_merge[i])
        h = h + 0.5 * swiglu(rmsnorm(h, ffn2_norm[i]), ffn2_w_gate[i], ffn2_w_up[i], ffn2_w_down[i])
    h = rmsnorm(h, final_norm)
    return h[:, N_META:]


def setup_inputs(seed: int = 0) -> dict:
    key = jax.random.key(seed)
    ks = jax.random.split(key, 24)

    def nrm(k, shape, scale):
        return jax.random.normal(k, shape, dtype=jnp.float32) * scale

    def gain(k, shape):
        return 1.0 + nrm(k, shape, 0.01)

    return {
        "x_prompt": nrm(ks[0], (BATCH, SEQ, D_MODEL), 1.0),
        "x_sample": nrm(ks[1], (DEC_BATCH, DEC_SEQ, D_MODEL), 1.0),
        "meta_tokens": nrm(ks[2], (N_META, D_MODEL), 1.0),
        "ffn1_norm": gain(ks[3], (DEPTH, D_MODEL)),
        "ffn1_w_gate": nrm(ks[4], (DEPTH, D_MODEL, D_FF), D_MODEL ** -0.5),
        "ffn1_w_up": nrm(ks[5], (DEPTH, D_MODEL, D_FF), D_MODEL ** -0.5),
        "ffn1_w_down": nrm(ks[6], (DEPTH, D_FF, D_MODEL), D_FF ** -0.5),
        "mix_norm": gain(ks[7], (DEPTH, D_MODEL)),
        "w_in": nrm(ks[8], (DEPTH, D_MODEL, IN_WIDTH), D_MODEL ** -0.5),
        "conv_w": nrm(ks[9], (DEPTH, CONV_K, CONV_WIDTH), CONV_K ** -0.5),
        "conv_b": nrm(ks[10], (DEPTH, CONV_WIDTH), 0.01),
        "q_norm": gain(ks[11], (DEPTH, HEAD_DIM)),
        "k_norm": gain(ks[12], (DEPTH, HEAD_DIM)),
        "w_o_attn": nrm(ks[13], (DEPTH, ATTN_WIDTH, D_MODEL), ATTN_WIDTH ** -0.5),
        "w_o_conv": nrm(ks[14], (DEPTH, CONV_WIDTH, D_MODEL), CONV_WIDTH ** -0.5),
        "w_merge": nrm(ks[15], (DEPTH, D_MODEL, D_MODEL), D_MODEL ** -0.5),
        "ffn2_norm": gain(ks[16], (DEPTH, D_MODEL)),
        "ffn2_w_gate": nrm(ks[17], (DEPTH, D_MODEL, D_FF), D_MODEL ** -0.5),
        "ffn2_w_up": nrm(ks[18], (DEPTH, D_MODEL, D_FF), D_MODEL ** -0.5),
        "ffn2_w_down": nrm(ks[19], (DEPTH, D_FF, D_MODEL), D_FF ** -0.5),
        "final_norm": gain(ks[20], (D_MODEL,)),
    }


def reference(x_prompt, x_sample, meta_tokens, ffn1_norm, ffn1_w_gate, ffn1_w_up, ffn1_w_down, mix_norm,
              w_in, conv_w, conv_b, q_norm, k_norm, w_o_attn, w_o_conv, w_merge, ffn2_norm, ffn2_w_gate,
              ffn2_w_up, ffn2_w_down, final_norm):
    y_prompt = trunk(x_prompt, meta_tokens, ffn1_norm, ffn1_w_gate, ffn1_w_up, ffn1_w_down, mix_norm, w_in,
                     conv_w, conv_b, q_norm, k_norm, w_o_attn, w_o_conv, w_merge, ffn2_norm, ffn2_w_gate,
                     ffn2_w_up, ffn2_w_down, final_norm)
    y_sample = trunk(x_sample, meta_tokens, ffn1_norm, ffn1_w_gate, ffn1_w_up, ffn1_w_down, mix_norm, w_in,
                     conv_w, conv_b, q_norm, k_norm, w_o_attn, w_o_conv, w_merge, ffn2_norm, ffn2_w_gate,
                     ffn2_w_up, ffn2_w_down, final_norm)
    return (y_prompt, y_sample)
```

```python
import numpy as np
from contextlib import ExitStack
import concourse.bass as bass
import concourse.mybir as mybir
from concourse.bass_utils import run_bass_kernel_spmd

F32 = mybir.dt.float32
BF16 = mybir.dt.bfloat16
AF = mybir.ActivationFunctionType
ALU = mybir.AluOpType

D = 1024
DFF = 2816
NL = 4
NREAL = 6144
NT = 6192
NTY = 6198
INW = 6656
NEG = -30000.0
VPL = 58
NV = NL * VPL + 8 + 4
GF_BASE = NL * VPL
FP_COL = NL * VPL + 8
EPS_COL = FP_COL + 2
ZERO_COL = FP_COL + 3
QCH = [(0, 6), (6, 6), (12, 5), (17, 5)]
NRING = 4

WSHAPES = [
    ("ffn1_w_gate", [NL, D, DFF]), ("ffn1_w_up", [NL, D, DFF]), ("ffn1_w_down", [NL, DFF, D]),
    ("w_in", [NL, D, INW]), ("w_o_attn", [NL, D, D]), ("w_o_conv", [NL, D, D]), ("w_merge", [NL, D, D]),
    ("ffn2_w_gate", [NL, D, DFF]), ("ffn2_w_up", [NL, D, DFF]), ("ffn2_w_down", [NL, DFF, D]),
]


def ycol(c):
    if c < 4096:
        return 17 + c
    if c < 6144:
        return c + 53
    if c < 6160:
        return 1 + (c - 6144)
    if c < 6176:
        return 4115 + (c - 6160)
    return 4133 + (c - 6176)


class Tracker:
    def __init__(self, nc, es):
        self.nc, self.es = nc, es
        self.E = {"pe": nc.tensor, "act": nc.scalar, "dve": nc.vector, "pool": nc.gpsimd, "sp": nc.sync}
        self.sems, self.cnt, self.waited = {}, {}, {}
        self.lw, self.rd = {}, {}
        self.nsem = 0
        for e in ("pe", "act", "dve", "pool"):
            self._sem(e)

    def _sem(self, name):
        if name not in self.sems:
            self.nsem += 1
            self.sems[name] = self.es.enter_context(self.nc.semaphore("sm%d" % self.nsem))
            self.cnt[name] = 0
        return self.sems[name]

    def _deps(self, eng, reads, writes):
        d = {}

        def add(x, raw):
            if x is None:
                return
            sem, val = x
            if sem == eng and (eng == "pe" or not raw):
                return
            if d.get(sem, 0) < val:
                d[sem] = val

        for k in reads:
            add(self.lw.get(k), True)
        for k in writes:
            add(self.lw.get(k), False)
            for r in self.rd.get(k, {}).items():
                add(r, False)
        return d

    def _need(self, eng, d):
        for sem, val in d.items():
            if self.waited.get((eng, sem), 0) < val:
                self.E[eng].wait_ge(self.sems[sem], val)
                self.waited[(eng, sem)] = val

    def _done(self, sem, val, reads, writes):
        for k in writes:
            self.lw[k] = (sem, val)
            self.rd[k] = {}
        for k in reads:
            r = self.rd.setdefault(k, {})
            if r.get(sem, 0) < val:
                r[sem] = val

    def op(self, eng, reads, writes, fn):
        self._need(eng, self._deps(eng, reads, writes))
        ins = fn()
        self.cnt[eng] += 1
        ins.then_inc(self.sems[eng], 1)
        self._done(eng, self.cnt[eng], reads, writes)

    def dma(self, q, semkey, reads, writes, out, in_, slow=False):
        self._sem(semkey)
        self._need(q, self._deps(q, reads, writes))
        if slow:
            ins = self.E[q].dma_start(out=out, in_=in_, allow_slow_non_contiguous=True)
        else:
            ins = self.E[q].dma_start(out=out, in_=in_)
        self.cnt[semkey] += 16
        ins.then_inc(self.sems[semkey], 16)
        self._done(semkey, self.cnt[semkey], reads, writes)

    def barrier(self):
        for e in ("pe", "act", "dve", "pool", "sp"):
            d = {s: c for s, c in self.cnt.items() if c > 0 and s != e}
            self._need(e, d)
        self.lw, self.rd = {}, {}


def build_nc(nlayers=NL, dbg=False):
    nc = bass.Bass("TRN2", target_bir_lowering=False)
    dbgT = nc.dram_tensor("dbgT", [D, 48], F32, kind="ExternalOutput").ap() if dbg else None
    dbgA = nc.dram_tensor("dbgA", [D, 48], F32, kind="ExternalOutput").ap() if dbg else None

    def din(name, shape, dt=F32):
        return nc.dram_tensor(name, shape, dt, kind="ExternalInput").ap()

    def dscr(name, shape, dt):
        return nc.dram_tensor(name, shape, dt, kind="Internal").ap()

    xT = din("xT", [D, NREAL])
    metaT = din("metaT", [D, 16])
    Wd = {n: din(n, shp) for n, shp in WSHAPES}
    vecs_d = din("vecs", [128, NV])
    mask_d = din("mask", [128, 67])
    cos_d = din("cosT", [128, NT])
    sin_d = din("sinT", [128, NT])
    perm_d = din("perm", [128, 128])
    ones_d = din("onesm", [128, 128])
    bones_d = din("bones", [128, 128])
    zeros_d = din("zeros", [128, 8])
    yT = nc.dram_tensor("yT", [D, NREAL], F32, kind="ExternalOutput").ap()
    hT = dscr("hT", [D, NT], F32)
    qT = dscr("qT", [D, NT], BF16)
    aT = dscr("aT", [D, NT], BF16)
    yS = dscr("yS", [2, D, NTY], BF16)
    cbT = dscr("cbT", [D, NT], BF16)
    sgaT = dscr("sgaT", [D, NT], BF16)
    sgcT = dscr("sgcT", [D, NT], BF16)

    def fm(ap):
        return ap.rearrange("(k p) n -> p k n", p=128)

    groups = []
    for g in range(6):
        tiles = [(0, 512), (512, 512)]
        if g == 5:
            tiles.append((1024, 48))
        groups.append(dict(g=g, G0=g * 1024, tiles=tiles, W=sum(w for _, w in tiles)))

    with ExitStack() as es:
        def sb(name, shape, dt):
            return es.enter_context(nc.sbuf_tensor(name, shape, dt))

        KT = sb("KT", [128, 2, NT], BF16)
        VX = sb("VX", [128, 50, 384], BF16)
        h = sb("h", [128, 8, 1072], F32)
        xn = sb("xn", [128, 8, 1072], BF16)
        actb = sb("actb", [128, 6, 1072], BF16)
        ring = sb("ring", [128, NRING, 4096], BF16)
        CS = sb("CS", [128, 2, 1072], F32)
        tmpf = sb("tmpf", [128, 10, 512], F32)
        stg = sb("stg", [128, 12, 514], BF16)
        vec = sb("vec", [128, NV], F32)
        msk = sb("msk", [128, 67], F32)
        perm = sb("perm_s", [128, 128], F32)
        ones = sb("ones_s", [128, 128], BF16)
        bones = sb("bones_s", [128, 128], BF16)
        hB = sb("hB", [128, 8], BF16)
        Cded = sb("Cded", [128, 8, 512], BF16)
        zt = sb("zt", [128, 8], BF16)
        ps = es.enter_context(nc.psum_tensor("ps", [128, 8, 512], F32))
        tr = Tracker(nc, es)
        st_ = {"bk": 0, "st": 0, "ring": 0, "sg": 0}

        def bk():
            b = st_["bk"]
            st_["bk"] = (b + 1) % 8
            return b

        def bk2():
            if st_["bk"] % 2:
                st_["bk"] = (st_["bk"] + 1) % 8
            b = st_["bk"]
            st_["bk"] = (b + 2) % 8
            return b

        def stn():
            s = st_["st"]
            st_["st"] = (s + 1) % 12
            return s

        def sgn():
            s = st_["sg"]
            st_["sg"] = (s + 1) % 3
            return 4 + s

        def ringv(s, nk, bw):
            return ring[:, s, 0:nk * bw].rearrange("p (k m) -> p k m", m=bw)

        def wload(src, nk, bw):
            s = st_["ring"]
            st_["ring"] = (s + 1) % NRING
            tr.dma("pool", ("ring", s), [], [("ring", s)], out=ringv(s, nk, bw), in_=src)
            return s

        mm = nc.tensor.matmul

        def pe_group(wfn, nk, sfn, tiles, banks, reads):
            def fn():
                last = None
                for k in range(nk):
                    for i, (c0, w) in enumerate(tiles):
                        last = mm(ps[:, banks[i], 0:w], lhsT=wfn(k), rhs=sfn(k, c0, w), start=(k == 0), stop=(k == nk - 1))
                return last
            tr.op("pe", reads, [("ps", b) for b in banks], fn)

        tr.dma("sp", "c0", [], ["vec"], out=vec[:], in_=vecs_d)
        tr.dma("sp", "c0", [], ["msk"], out=msk[:], in_=mask_d)
        tr.dma("sp", "c0", [], ["perm"], out=perm[:], in_=perm_d)
        tr.dma("pool", "c1", [], ["ones"], out=ones[:], in_=ones_d)
        tr.dma("pool", "c1", [], ["bones"], out=bones[:], in_=bones_d)
        tr.dma("pool", "c1", [], ["zt"], out=zt[:], in_=zeros_d)
        tr.op("dve", [], ["VX"], lambda: nc.vector.memset(VX[:], 1.0))
        tr.barrier()
        for par in range(2):
            for gc in (0, 4113, 4114, 4131, 4132, 6197):
                tr.dma("sp", "c2", ["zt"], [("yS", "guard", par, gc)], out=fm(yS[par])[:, :, gc:gc + 1], in_=zt[:].rearrange("p (k o) -> p k o", o=1), slow=True)
        tr.barrier()

        xn_keys = [("xn", k, hf) for k in range(8) for hf in range(2)]

        def xk(k, c0):
            return ("xn", k, 0 if c0 < 512 else 1)
        h_keys = [("h", k) for k in range(8)]

        def norm(g, gvbase, dst_xn=True):
            for (c0, w) in g["tiles"]:
                b = bk()
                for k in range(8):
                    s = stn()
                    tr.op("act", [("h", k)], [("st", s)], lambda: nc.scalar.activation(out=stg[:, s, 0:w], in_=h[:, k, c0:c0 + w], func=AF.Square))
                    tr.op("pe", [("st", s), "ones"], [("ps", b)], lambda: mm(ps[:, b, 0:w], lhsT=ones[:], rhs=stg[:, s, 0:w], start=(k == 0), stop=(k == 7)))
                tr.op("act", [("ps", b), "vec"], [("tf", 0)], lambda: nc.scalar.activation(out=tmpf[:, 0, 0:w], in_=ps[:, b, 0:w], func=AF.Sqrt, bias=vec[:, EPS_COL:EPS_COL + 1], scale=1.0 / D))
                tr.op("dve", [("tf", 0)], [("tf", 0)], lambda: nc.vector.reciprocal(out=tmpf[:, 0, 0:w], in_=tmpf[:, 0, 0:w]))
                for k in range(8):
                    if dst_xn:
                        tr.op("dve", [("h", k), ("tf", 0), "vec"], [xk(k, c0)], lambda: nc.vector.scalar_tensor_tensor(
                            out=xn[:, k, c0:c0 + w], in0=h[:, k, c0:c0 + w], scalar=vec[:, gvbase + k:gvbase + k + 1], in1=tmpf[:, 0, 0:w], op0=ALU.mult, op1=ALU.mult))
                    else:
                        tr.op("dve", [("h", k), ("tf", 0), "vec"], [("h", k)], lambda: nc.vector.scalar_tensor_tensor(
                            out=h[:, k, c0:c0 + w], in0=h[:, k, c0:c0 + w], scalar=vec[:, gvbase + k:gvbase + k + 1], in1=tmpf[:, 0, 0:w], op0=ALU.mult, op1=ALU.mult))

        def ffn(g, l, wg, wu, wdn, drain_jobs=False):
            tiles = g["tiles"]
            wgf = Wd[wg][l].rearrange("(k p) m -> p k m", p=128)
            wuf = Wd[wu][l].rearrange("(k p) m -> p k m", p=128)
            wdf = Wd[wdn][l].rearrange("(f p) m -> p f m", p=128)
            for (f0, nf) in QCH:
                for (b0, nb) in ((0, 3), (3, nf - 3)):
                    bw = nb * 128
                    cs = (f0 + b0) * 128
                    s_g = wload(wgf[:, :, cs:cs + bw], 8, bw)
                    s_u = wload(wuf[:, :, cs:cs + bw], 8, bw)
                    for j in range(nb):
                        fs = b0 + j
                        gb = [bk() for _ in tiles]
                        ub = [bk() for _ in tiles]
                        pe_group(lambda k: ringv(s_g, 8, bw)[:, k, j * 128:(j + 1) * 128], 8, lambda k, c0, w: xn[:, k, c0:c0 + w], tiles, gb, xn_keys + [("ring", s_g)])
                        pe_group(lambda k: ringv(s_u, 8, bw)[:, k, j * 128:(j + 1) * 128], 8, lambda k, c0, w: xn[:, k, c0:c0 + w], tiles, ub, xn_keys + [("ring", s_u)])
                        for i, (c0, w) in enumerate(tiles):
                            t = sgn()
                            tr.op("act", [("ps", gb[i])], [("tf", t)], lambda: nc.scalar.activation(out=tmpf[:, t, 0:w], in_=ps[:, gb[i], 0:w], func=AF.Silu))
                            tr.op("dve", [("ps", ub[i]), ("tf", t)], [("act", fs)], lambda: nc.vector.tensor_tensor(out=actb[:, fs, c0:c0 + w], in0=ps[:, ub[i], 0:w], in1=tmpf[:, t, 0:w], op=ALU.mult))
                        if drain_jobs:
                            drain(1)
                for mb in range(2):
                    s_d = wload(wdf[:, f0:f0 + nf, mb * 512:(mb + 1) * 512], nf, 512)
                    for j in range(4):
                        m = mb * 4 + j
                        ob = [bk() for _ in tiles]
                        pe_group(lambda fc: ringv(s_d, nf, 512)[:, fc, j * 128:(j + 1) * 128], nf, lambda fc, c0, w: actb[:, fc, c0:c0 + w], tiles, ob,
                                 [("act", fc) for fc in range(nf)] + [("ring", s_d)])
                        for i, (c0, w) in enumerate(tiles):
                            tr.op("dve", [("ps", ob[i]), ("h", m)], [("h", m)], lambda: nc.vector.scalar_tensor_tensor(
                                out=h[:, m, c0:c0 + w], in0=ps[:, ob[i], 0:w], scalar=0.5, in1=h[:, m, c0:c0 + w], op0=ALU.mult, op1=ALU.add))

        def store_cols(dst_fm, key, chunk, s, g, c0, w):
            col = g["G0"] + c0
            tr.dma("sp", ("st", s), [("st", s)], [(key, chunk, col)], out=dst_fm[:, chunk, col:col + w], in_=stg[:, s, 0:w])

        def qk_chunk(g, l, bs, tiles, gcol, is_q, chunk):
            n = len(tiles)
            T = [(1 + 3 * i, 2 + 3 * i, 3 + 3 * i) for i in range(n)]
            sq = []
            for i, (c0, w) in enumerate(tiles):
                t1 = T[i][0]
                b = bs[i]
                tr.op("act", [("ps", b)], [("tf", t1)], lambda: nc.scalar.activation(out=tmpf[:, t1, 0:w], in_=ps[:, b, 0:w], func=AF.Copy))
                s_ = stn()
                sq.append(s_)
                tr.op("dve", [("tf", t1)], [("st", s_)], lambda: nc.vector.tensor_tensor(out=stg[:, s_, 0:w], in0=tmpf[:, t1, 0:w], in1=tmpf[:, t1, 0:w], op=ALU.mult))
            yield
            b2s = []
            for i, (c0, w) in enumerate(tiles):
                b2 = bk()
                b2s.append(b2)
                s_ = sq[i]
                tr.op("pe", [("st", s_), "bones"], [("ps", b2)], lambda: mm(ps[:, b2, 0:w], lhsT=bones[:], rhs=stg[:, s_, 0:w], start=True, stop=True))
            for i, (c0, w) in enumerate(tiles):
                t1, t2, t3 = T[i]
                b2 = b2s[i]
                tr.op("act", [("ps", b2), "vec"], [("tf", t2)], lambda: nc.scalar.activation(out=tmpf[:, t2, 0:w], in_=ps[:, b2, 0:w], func=AF.Sqrt, bias=vec[:, EPS_COL:EPS_COL + 1], scale=1.0 / 64))
                tr.op("dve", [("tf", t2)], [("tf", t2)], lambda: nc.vector.reciprocal(out=tmpf[:, t2, 0:w], in_=tmpf[:, t2, 0:w]))
                tr.op("dve", [("tf", t1), ("tf", t2), "vec"], [("tf", t1)], lambda: nc.vector.scalar_tensor_tensor(
                    out=tmpf[:, t1, 0:w], in0=tmpf[:, t1, 0:w], scalar=vec[:, gcol:gcol + 1], in1=tmpf[:, t2, 0:w], op0=ALU.mult, op1=ALU.mult))
            yield
            b3s = []
            for i, (c0, w) in enumerate(tiles):
                t1 = T[i][0]
                b3 = bk()
                b3s.append(b3)
                tr.op("pe", [("tf", t1), "perm"], [("ps", b3)], lambda: mm(ps[:, b3, 0:w], lhsT=perm[:], rhs=tmpf[:, t1, 0:w], start=True, stop=True))
            for i, (c0, w) in enumerate(tiles):
                t1, t2, t3 = T[i]
                b3 = b3s[i]
                tr.op("dve", [("tf", t1), "CS"], [("tf", t3)], lambda: nc.vector.tensor_tensor(out=tmpf[:, t3, 0:w], in0=tmpf[:, t1, 0:w], in1=CS[:, 0, c0:c0 + w], op=ALU.mult))
                tr.op("dve", [("ps", b3), "CS"], [("tf", t2)], lambda: nc.vector.tensor_tensor(out=tmpf[:, t2, 0:w], in0=ps[:, b3, 0:w], in1=CS[:, 1, c0:c0 + w], op=ALU.mult))
                if is_q:
                    s2 = stn()
                    tr.op("dve", [("tf", t2), ("tf", t3)], [("st", s2)], lambda: nc.vector.tensor_tensor(out=stg[:, s2, 0:w], in0=tmpf[:, t3, 0:w], in1=tmpf[:, t2, 0:w], op=ALU.add))
                    store_cols(fm(qT), "qT", chunk, s2, g, c0, w)
                else:
                    col = g["G0"] + c0
                    tr.op("dve", [("tf", t2), ("tf", t3)], [("KT", chunk)], lambda: nc.vector.tensor_tensor(out=KT[:, chunk, col:col + w], in0=tmpf[:, t3, 0:w], in1=tmpf[:, t2, 0:w], op=ALU.add))
            yield

        def w_in_pass(g, l):
            tiles = g["tiles"]
            G0, Wg = g["G0"], g["W"]
            win = Wd["w_in"][l]
            ysf = fm(yS[l % 2])
            winf = win.rearrange("(k p) m -> p k m", p=128)
            vb = l * VPL
            tr.dma("sp", "cs", [], ["CS"], out=CS[:, 0, 0:Wg], in_=cos_d[:, G0:G0 + Wg])
            tr.dma("sp", "cs", [], ["CS"], out=CS[:, 1, 0:Wg], in_=sin_d[:, G0:G0 + Wg])

            def main(s, j):
                bs = [bk() for _ in tiles]
                pe_group(lambda k: ringv(s, 8, 512)[:, k, j * 128:(j + 1) * 128], 8, lambda k, c0, w: xn[:, k, c0:c0 + w], tiles, bs, xn_keys + [("ring", s)])
                return bs

            def load_q(qb):
                s = st_["ring"]
                st_["ring"] = (s + 1) % NRING
                sv = ringv(s, 8, 512).rearrange("p k (c hf j) -> p k c hf j", c=4, hf=2, j=64)
                src = winf[:, :, qb * 512:(qb + 1) * 512].rearrange("p k (hf c j) -> p k c hf j", hf=2, c=4, j=64)
                for hf in range(2):
                    for cl in range(4):
                        tr.dma("pool", ("ring", s), [], [("ring", s)], out=sv[:, :, cl, hf, :], in_=src[:, :, cl, hf, :])
                return s

            def evac_simple(bs, func, dstT, key, chunk):
                for i, (c0, w) in enumerate(tiles):
                    s2 = stn()
                    tr.op("act", [("ps", bs[i])], [("st", s2)], lambda: nc.scalar.activation(out=stg[:, s2, 0:w], in_=ps[:, bs[i], 0:w], func=func))
                    store_cols(fm(dstT), key, chunk, s2, g, c0, w)

            for qb in range(2):
                s_q = load_q(qb)
                s_cb = wload(winf[:, :, 1536 + qb * 512:1536 + (qb + 1) * 512], 8, 512)
                s_ga = wload(winf[:, :, 4608 + qb * 512:4608 + (qb + 1) * 512], 8, 512)
                for j in range(4):
                    chunk = qb * 4 + j
                    bs = main(s_q, j)
                    st1 = qk_chunk(g, l, bs, tiles, vb + 56, True, chunk)
                    next(st1)
                    b1 = main(s_cb, j)
                    evac_simple(b1, AF.Copy, cbT, "cbT", chunk)
                    next(st1)
                    b2 = main(s_ga, j)
                    evac_simple(b2, AF.Sigmoid, sgaT, "sgaT", chunk)
                    next(st1)
            s = wload(winf[:, :, 1024:1536], 8, 512)
            s_gc = wload(winf[:, :, 5632:5632 + 512], 8, 512)
            for j in range(2):
                bs = main(s, j)
                st1 = qk_chunk(g, l, bs, tiles, vb + 57, False, j)
                next(st1)
                b1 = main(s_gc, 2 * j)
                evac_simple(b1, AF.Sigmoid, sgcT, "sgcT", 2 * j)
                next(st1)
                b2 = main(s_gc, 2 * j + 1)
                evac_simple(b2, AF.Sigmoid, sgcT, "sgcT", 2 * j + 1)
                next(st1)
            vblocks = [(j * 128, 128, (G0 + j * 128) // 128) for j in range(8)]
            if g["g"] == 5:
                vblocks += [(1024, 32, 48), (1056, 16, 49)]
            for (c0, nr, kc) in vblocks:
                b = bk()

                def fnv():
                    last = None
                    for k in range(8):
                        last = mm(ps[0:nr, b, 0:256], lhsT=xn[:, k, c0:c0 + nr], rhs=ringv(s, 8, 512)[:, k, 256:512], start=(k == 0), stop=(k == 7))
                    return last
                tr.op("pe", xn_keys + [("ring", s)], [("ps", b)], fnv)

                def fnc():
                    last = None
                    for hd in range(4):
                        dst = (hd // 2) * 192 + (hd % 2) * 128
                        last = nc.vector.tensor_copy(out=VX[0:nr, kc, dst:dst + 64], in_=ps[0:nr, b, hd * 64:(hd + 1) * 64])
                    return last
                tr.op("dve", [("ps", b)], [("VX", kc)], fnc)
            s_gc = wload(winf[:, :, 5632 + 512:5632 + 1024], 8, 512)
            for j in range(4):
                b1 = main(s_gc, j)
                evac_simple(b1, AF.Sigmoid, sgcT, "sgcT", 4 + j)
            for blk in range(2):
                s_c = wload(winf[:, :, 2560 + blk * 512:2560 + (blk + 1) * 512], 8, 512)
                s_x = wload(winf[:, :, 3584 + blk * 512:3584 + (blk + 1) * 512], 8, 512)
                for j in range(4):
                    chunk = blk * 4 + j
                    bc = main(s_c, j)
                    bx = main(s_x, j)
                    for i, (c0, w) in enumerate(tiles):
                        t = sgn()
                        tr.op("act", [("ps", bc[i])], [("tf", t)], lambda: nc.scalar.activation(out=tmpf[:, t, 0:w], in_=ps[:, bc[i], 0:w], func=AF.Copy))
                        s2 = stn()
                        tr.op("dve", [("ps", bx[i]), ("tf", t)], [("st", s2)], lambda: nc.vector.tensor_tensor(out=stg[:, s2, 0:w], in0=ps[:, bx[i], 0:w], in1=tmpf[:, t, 0:w], op=ALU.mult))
                        col = G0 + c0
                        if w == 48:
                            for sgm in range(3):
                                yc = ycol(col + sgm * 16)
                                tr.dma("sp", ("st", s2), [("st", s2)], [("yS", chunk, "m", sgm)], out=ysf[:, chunk, yc:yc + 16], in_=stg[:, s2, sgm * 16:(sgm + 1) * 16])
                        else:
                            yc = ycol(col)
                            tr.dma("sp", ("st", s2), [("st", s2)], [("yS", chunk, col)], out=ysf[:, chunk, yc:yc + w], in_=stg[:, s2, 0:w])
                            if col == 2048:
                                tr.dma("sp", ("st", s2), [("st", s2)], [("yS", chunk, "b15r")], out=ysf[:, chunk, 4131:4132], in_=stg[:, s2, 0:1], slow=True)

        A_t = xn[:, :, 0:512]
        C_xn = xn[:, :, 536:1048]
        M_t = actb[:, :, :].rearrange("p f n -> p (f n)")[:, 0:4096].rearrange("p (k n) -> p k n", n=512)
        a_keys = [("xn", k, 0) for k in range(8)]
        m_keys = [("act", f) for f in range(6)]
        pending = []

        def cbuf(ti):
            if ti == 1:
                return C_xn, [("xn", k, 1) for k in range(8)]
            return Cded, ["Cd"]

        def conv_jobs(g, l, ti):
            G0 = g["G0"]
            vb = l * VPL
            cbf = fm(cbT)
            ysf = fm(yS[l % 2])
            (c0, w) = g["tiles"][ti]
            col = G0 + c0
            Cb, ckeys = cbuf(ti)
            if w == 48:
                segs = [(0, 16, 0), (16, 16, 4114), (32, 16, 4132)]
            else:
                segs = [(0, w, ycol(col) - 1)]

            def job(c):
                if c == 0 and col == 2048:
                    tr.dma("sp", "hb", [("yS", "any")], ["hB"], out=hB[:].rearrange("p (k o) -> p k o", o=1), in_=ysf[:, :, 4130:4131], slow=True)
                for (o0, sw, y0) in segs:
                    sy = stn()
                    tr.dma("sp", ("st", sy), [("yS", "any")], [("st", sy)], out=stg[:, sy, 0:sw + 2], in_=ysf[:, c, y0:y0 + sw + 2])
                    sc = stn()
                    tr.dma("sp", ("st", sc), [("cbT", "any")], [("st", sc)], out=stg[:, sc, 0:sw], in_=cbf[:, c, col + o0:col + o0 + sw])
                    if col == 1536:
                        tr.op("dve", [("st", sy), "vec"], [("st", sy)], lambda: nc.vector.tensor_scalar(
                            out=stg[:, sy, sw + 1:sw + 2], in0=stg[:, sy, sw + 1:sw + 2], scalar1=vec[:, FP_COL:FP_COL + 1], scalar2=None, op0=ALU.mult))
                    if col == 2048:
                        tr.op("dve", [("st", sy), "vec"], [("st", sy)], lambda: nc.vector.tensor_scalar(
                            out=stg[:, sy, 0:1], in0=stg[:, sy, 0:1], scalar1=vec[:, FP_COL:FP_COL + 1], scalar2=None, op0=ALU.mult))
                        tr.op("dve", [("st", sy), "vec", "hB"], [("st", sy)], lambda: nc.vector.scalar_tensor_tensor(
                            out=stg[:, sy, 0:1], in0=hB[:, c:c + 1], scalar=vec[:, FP_COL + 1:FP_COL + 2], in1=stg[:, sy, 0:1], op0=ALU.mult, op1=ALU.add))
                    tr.op("dve", [("st", sy), "vec"], [("tf", 3)], lambda: nc.vector.tensor_scalar(
                        out=tmpf[:, 3, 0:sw], in0=stg[:, sy, 0:sw], scalar1=vec[:, vb + 24 + c:vb + 25 + c], scalar2=None, op0=ALU.mult))
                    tr.op("dve", [("st", sy), "vec", ("tf", 3)], [("tf", 3)], lambda: nc.vector.scalar_tensor_tensor(
                        out=tmpf[:, 3, 0:sw], in0=stg[:, sy, 1:sw + 1], scalar=vec[:, vb + 32 + c:vb + 33 + c], in1=tmpf[:, 3, 0:sw], op0=ALU.mult, op1=ALU.add))
                    tr.op("dve", [("st", sy), "vec", ("tf", 3)], [("tf", 3)], lambda: nc.vector.scalar_tensor_tensor(
                        out=tmpf[:, 3, 0:sw], in0=stg[:, sy, 2:sw + 2], scalar=vec[:, vb + 40 + c:vb + 41 + c], in1=tmpf[:, 3, 0:sw], op0=ALU.mult, op1=ALU.add))
                    tr.op("dve", [("st", sc), "vec", ("tf", 3)], ckeys, lambda: nc.vector.scalar_tensor_tensor(
                        out=Cb[:, c, o0:o0 + sw], in0=tmpf[:, 3, 0:sw], scalar=vec[:, vb + 48 + c:vb + 49 + c], in1=stg[:, sc, 0:sw], op0=ALU.add, op1=ALU.mult))
            return [(lambda c=c: job(c)) for c in range(8)]

        def drain(n=None):
            k = 0
            while pending and (n is None or k < n):
                pending.pop(0)()
                k += 1

        def proj_tile(g, l, ti):
            G0 = g["G0"]
            (c0, w) = g["tiles"][ti]
            col = G0 + c0
            aTf, sgaf, sgcf = fm(aT), fm(sgaT), fm(sgcT)
            woa = Wd["w_o_attn"][l]
            wocf = Wd["w_o_conv"][l].rearrange("(k p) m -> p k m", p=128)
            wmf = Wd["w_merge"][l].rearrange("(k p) m -> p k m", p=128)
            Cb, ckeys = cbuf(ti)
            tr.dma("sp", "aload", [("aT", col)], a_keys, out=A_t[:, :, 0:w], in_=aTf[:, :, col:col + w])
            for blk in range(2):
                s_a = st_["ring"]
                st_["ring"] = (s_a + 1) % NRING
                for hf in range(2):
                    for cb_ in range(2):
                        src = woa[:, blk * 512:(blk + 1) * 512].rearrange("(cb hf cl r) m -> cb hf r cl m", cb=2, hf=2, cl=4, r=64)[cb_, hf]
                        dst = ringv(s_a, 8, 512)[hf * 64:(hf + 1) * 64, cb_ * 4:(cb_ + 1) * 4, :]
                        tr.dma("pool", ("ring", s_a), [], [("ring", s_a)], out=dst, in_=src)
                s_c = wload(wocf[:, :, blk * 512:(blk + 1) * 512], 8, 512)
                for j in range(4):
                    m = blk * 4 + j
                    ba, bc = bk(), bk()
                    pe_group(lambda k: ringv(s_a, 8, 512)[:, k, j * 128:(j + 1) * 128], 8, lambda k, cc0, ww: A_t[:, k, 0:ww], [(0, w)], [ba], a_keys + [("ring", s_a)])
                    pe_group(lambda k: ringv(s_c, 8, 512)[:, k, j * 128:(j + 1) * 128], 8, lambda k, cc0, ww: Cb[:, k, 0:ww], [(0, w)], [bc], ckeys + [("ring", s_c)])
                    sa, sc2 = stn(), stn()
                    tr.dma("sp", ("st", sa), [("sgaT", "any")], [("st", sa)], out=stg[:, sa, 0:w], in_=sgaf[:, m, col:col + w])
                    tr.dma("sp", ("st", sc2), [("sgcT", "any")], [("st", sc2)], out=stg[:, sc2, 0:w], in_=sgcf[:, m, col:col + w])
                    tr.op("dve", [("ps", ba), ("st", sa)], [("tf", 1)], lambda: nc.vector.tensor_tensor(out=tmpf[:, 1, 0:w], in0=ps[:, ba, 0:w], in1=stg[:, sa, 0:w], op=ALU.mult))
                    tr.op("dve", [("ps", bc), ("st", sc2)], [("tf", 2)], lambda: nc.vector.tensor_tensor(out=tmpf[:, 2, 0:w], in0=ps[:, bc, 0:w], in1=stg[:, sc2, 0:w], op=ALU.mult))
                    tr.op("dve", [("tf", 1), ("tf", 2)], m_keys, lambda: nc.vector.tensor_tensor(out=M_t[:, m, 0:w], in0=tmpf[:, 1, 0:w], in1=tmpf[:, 2, 0:w], op=ALU.add))
                    drain(1)
            for blk in range(2):
                s_m = wload(wmf[:, :, blk * 512:(blk + 1) * 512], 8, 512)
                for j in range(4):
                    m = blk * 4 + j
                    bo = bk()
                    pe_group(lambda k: ringv(s_m, 8, 512)[:, k, j * 128:(j + 1) * 128], 8, lambda k, cc0, ww: M_t[:, k, 0:ww], [(0, w)], [bo], m_keys + [("ring", s_m)])
                    tr.op("dve", [("ps", bo), ("h", m)], [("h", m)], lambda: nc.vector.tensor_tensor(out=h[:, m, c0:c0 + w], in0=ps[:, bo, 0:w], in1=h[:, m, c0:c0 + w], op=ALU.add))

        def p2_front(g, l, prefetched):
            nt = len(g["tiles"])
            if not prefetched:
                pending.extend(conv_jobs(g, l, 0))
            drain()
            for ti in range(nt):
                if ti + 1 < nt:
                    pending.extend(conv_jobs(g, l, ti + 1))
                proj_tile(g, l, ti)
                drain()

        def attention(l):
            qTf, aTf = fm(qT), fm(aT)
            Pb = [actb[:, i, 0:1024] for i in range(2)]
            Ao = actb[:, :, :].rearrange("p f n -> p (f n)")[:, 2144:2144 + 4096].rearrange("p (k n) -> p k n", n=512)
            Qb = [xn[:, :, 0:512], xn[:, :, 536:1048]]
            slots = []
            kch0 = [(j * 128, 128, j, None) for j in range(32)] + [(6144, 32, 48, None)]
            qt0 = [(j * 512, 512, j // 4) for j in range(8)] + [(6144, 16, 0), (6160, 16, 1)]
            kch1 = [(4096 + j * 128, 128, 32 + j, None) for j in range(16)] + [(6176, 16, 49, None)]
            qt1 = [(4096 + j * 512, 512, -1) for j in range(4)] + [(6176, 16, -1)]
            xyc = [0]
            items = [(qc, w, half, kch0) for (qc, w, half) in qt0] + [(qc, w, half, kch1) for (qc, w, half) in qt1]

            def qload(n):
                (qc_, w_, _, _) = items[n]
                tr.dma("sp", ("Q", n % 2), [("qT", "any")], [("Q", n % 2)], out=Qb[n % 2][:, :, 0:w_], in_=qTf[:, :, qc_:qc_ + w_])
            qload(0)
            for qi, (qc, w, half, kchs) in enumerate(items):
                if True:
                    Q = Qb[qi % 2]
                    qkey = ("Q", qi % 2)
                    if qi + 1 < len(items):
                        qload(qi + 1)
                    for c in range(8):
                        kchunk = c // 4
                        pair = c // 4
                        bX, bY = (4, 5) if (xyc[0] % 2 == 0) else (6, 7)
                        xyc[0] += 1
                        nk = len(kchs)
                        state = {}

                        def s_step(i):
                            (kc0, nr, vxc, _) = kchs[i]
                            b2 = 2 * (i % 2)
                            state[i] = b2

                            def fn():
                                mm(ps[0:nr, b2, 0:w], lhsT=KT[0:64, kchunk, kc0:kc0 + nr], rhs=Q[0:64, c, 0:w], start=True, stop=True)
                                return mm(ps[0:nr, b2 + 1, 0:w], lhsT=KT[64:128, kchunk, kc0:kc0 + nr], rhs=Q[64:128, c, 0:w], start=True, stop=True)
                            tr.op("pe", [qkey, ("KT", kchunk)], [("ps", b2), ("ps", b2 + 1)], fn)

                        def e_step(i):
                            (kc0, nr, vxc, _) = kchs[i]
                            b2 = state[i]
                            P = Pb[i % 2]
                            pk = ("P", i % 2)
                            mc = ZERO_COL_M if half < 0 else half * 33 + i
                            tr.op("act", [("ps", b2), ("ps", b2 + 1), "msk"], [pk], lambda: nc.scalar.activation(
                                out=P[0:nr, :].rearrange("p (t n) -> p t n", t=2)[:, :, 0:w], in_=ps[0:nr, b2:b2 + 2, 0:w], func=AF.Exp, bias=msk[0:nr, mc:mc + 1], scale=0.125))

                        def v_step(i):
                            (kc0, nr, vxc, _) = kchs[i]
                            P = Pb[i % 2]
                            pk = ("P", i % 2)

                            def fn():
                                mm(ps[:, bX, 0:w], lhsT=VX[0:nr, vxc, pair * 192:pair * 192 + 128], rhs=P[0:nr, 0:w], start=(i == 0), stop=(i == nk - 1))
                                return mm(ps[:, bY, 0:w], lhsT=VX[0:nr, vxc, pair * 192 + 64:pair * 192 + 192], rhs=P[0:nr, 512:512 + w], start=(i == 0), stop=(i == nk - 1))
                            tr.op("pe", [pk, ("VX", vxc)], [("ps", bX), ("ps", bY)], fn)

                        s_step(0)
                        if nk > 1:
                            s_step(1)
                        for i in range(nk):
                            e_step(i)
                            if i + 2 < nk:
                                s_step(i + 2)
                            v_step(i)
                        def frec():
                            nc.vector.reciprocal(out=tmpf[64:128, 1, 0:w], in_=ps[64:128, bX, 0:w])
                            return nc.vector.reciprocal(out=tmpf[0:64, 1, 0:w], in_=ps[0:64, bY, 0:w])
                        tr.op("dve", [("ps", bX), ("ps", bY)], [("tf", 1)], frec)

                        def fmul():
                            nc.vector.tensor_tensor(out=Ao[0:64, c, 0:w], in0=ps[0:64, bX, 0:w], in1=tmpf[64:128, 1, 0:w], op=ALU.mult)
                            return nc.vector.tensor_tensor(out=Ao[64:128, c, 0:w], in0=ps[64:128, bY, 0:w], in1=tmpf[0:64, 1, 0:w], op=ALU.mult)
                        tr.op("dve", [("ps", bX), ("ps", bY), ("tf", 1)], ["Ao"], fmul)
                    tr.dma("sp", "astore", ["Ao"], [("aT", qc)], out=aTf[:, :, qc:qc + w], in_=Ao[:, :, 0:w])

        ZERO_COL_M = 66

        hTf = fm(hT)
        xTf = fm(xT)
        mTf = fm(metaT)
        yTf = fm(yT)
        for l in range(nlayers + 1):
            for g in groups:
                G0, Wg = g["G0"], g["W"]
                if l == 0:
                    tr.dma("sp", "hload", [], h_keys, out=h[:, :, 0:1024], in_=xTf[:, :, G0:G0 + 1024])
                    if g["g"] == 5:
                        for sgm in range(3):
                            tr.dma("sp", "hload", [], h_keys, out=h[:, :, 1024 + sgm * 16:1040 + sgm * 16], in_=mTf)
                else:
                    tr.dma("sp", "hload", [("hT", g["g"])], h_keys, out=h[:, :, 0:Wg], in_=hTf[:, :, G0:G0 + Wg])
                    p2_front(g, l - 1, prefetched=(g["g"] > 0))
                    norm(g, (l - 1) * VPL + 16)
                    if g["g"] + 1 < len(groups):
                        pending.extend(conv_jobs(groups[g["g"] + 1], l - 1, 0))
                    ffn(g, l - 1, "ffn2_w_gate", "ffn2_w_up", "ffn2_w_down", drain_jobs=True)
                    drain()
                if l < nlayers:
                    norm(g, l * VPL + 0)
                    ffn(g, l, "ffn1_w_gate", "ffn1_w_up", "ffn1_w_down")
                    tr.dma("sp", "hstore", h_keys, [("hT", g["g"])], out=hTf[:, :, G0:G0 + Wg], in_=h[:, :, 0:Wg])
                    norm(g, l * VPL + 8)
                    w_in_pass(g, l)
                else:
                    norm(g, GF_BASE, dst_xn=False)
                    tr.dma("sp", "ostore", h_keys, [("yT", g["g"])], out=yTf[:, :, G0:G0 + 1024], in_=h[:, :, 0:1024])
                    if dbg and g["g"] == 5:
                        tr.dma("sp", "ostore", h_keys, [("dbg", 0)], out=fm(dbgT), in_=h[:, :, 1024:1072])
            if l < nlayers:
                tr.barrier()
                attention(l)
                tr.barrier()
        tr.barrier()
        if dbg:
            tr.dma("sp", "dbg1", [], [("xn", 0)], out=xn[:, :, 0:48], in_=fm(aT)[:, :, 6144:6192])
            tr.op("dve", [("xn", 0)], [("h", 0)], lambda: nc.vector.tensor_copy(out=h[:, :, 0:48], in_=xn[:, :, 0:48]))
            tr.dma("sp", "dbg1", [("h", 0)], [("dbgA", 0)], out=fm(dbgA), in_=h[:, :, 0:48])
            tr.barrier()
    return nc


def _host_prep(inputs):
    xp = np.asarray(inputs["x_prompt"], dtype=np.float32)
    xs = np.asarray(inputs["x_sample"], dtype=np.float32)
    meta = np.asarray(inputs["meta_tokens"], dtype=np.float32)
    metaT = np.ascontiguousarray(meta.T)
    vecs = np.zeros((128, NV), np.float32)

    def fmv(v):
        return np.asarray(v, np.float32).reshape(8, 128).T

    for l in range(NL):
        b = l * VPL
        vecs[:, b + 0:b + 8] = fmv(inputs["ffn1_norm"][l])
        vecs[:, b + 8:b + 16] = fmv(inputs["mix_norm"][l])
        vecs[:, b + 16:b + 24] = fmv(inputs["ffn2_norm"][l])
        vecs[:, b + 24:b + 32] = fmv(inputs["conv_w"][l][0])
        vecs[:, b + 32:b + 40] = fmv(inputs["conv_w"][l][1])
        vecs[:, b + 40:b + 48] = fmv(inputs["conv_w"][l][2])
        vecs[:, b + 48:b + 56] = fmv(inputs["conv_b"][l])
        vecs[:, b + 56] = np.tile(np.asarray(inputs["q_norm"][l], np.float32), 2)
        vecs[:, b + 57] = np.tile(np.asarray(inputs["k_norm"][l], np.float32), 2)
    vecs[:, GF_BASE:GF_BASE + 8] = fmv(inputs["final_norm"])
    vecs[:, EPS_COL] = 1e-6
    vecs[:, ZERO_COL] = 0.0
    perm = np.zeros((128, 128), np.float32)
    for m in range(128):
        partner = m + 16 if (m % 32) < 16 else m - 16
        perm[partner, m] = 1.0
    onesm = np.ones((128, 128), np.float32)
    bones = np.zeros((128, 128), np.float32)
    bones[:64, :64] = 1.0
    bones[64:, 64:] = 1.0
    zeros = np.zeros((128, 8), np.float32)
    inv_freq = (10000.0 ** (-np.arange(0, 32, 2, dtype=np.float32) / 32.0)).astype(np.float32)

    def rope_tables(rows, cols):
        p = np.arange(128)
        d = p % 64
        fi = d % 16
        isrow = d < 32
        neg = (d % 32) < 16
        pos = np.where(isrow[:, None], rows[None, :], cols[None, :]).astype(np.float32)
        ang = (pos * inv_freq[fi][:, None]).astype(np.float32)
        c = np.cos(ang).astype(np.float32)
        s = np.sin(ang).astype(np.float32)
        s = np.where(neg[:, None], -s, s).astype(np.float32)
        return np.ascontiguousarray(c), np.ascontiguousarray(s)

    tok4096 = np.arange(4096)
    tok2048 = np.arange(2048)
    mrow = np.full(16, -1.0, np.float32)
    mcol = np.arange(16, dtype=np.float32)
    in_maps = []
    for core in range(8):
        if core < 4:
            s0 = xp[core]
            s1 = xs[core]
            rows = np.concatenate([tok4096 // 64, tok2048 // 64, mrow, mrow, mrow]).astype(np.float32)
            cols = np.concatenate([tok4096 % 64, tok2048 % 64, mcol, mcol, mcol]).astype(np.float32)
            fp = 1.0
        else:
            j = core - 4
            s0 = np.concatenate([xs[4 + 3 * j], xs[5 + 3 * j]], axis=0)
            s1 = xs[6 + 3 * j]
            rows = np.concatenate([tok2048 // 64, tok2048 // 64, tok2048 // 64, mrow, mrow, mrow]).astype(np.float32)
            cols = np.concatenate([tok2048 % 64, tok2048 % 64, tok2048 % 64, mcol, mcol, mcol]).astype(np.float32)
            fp = 0.0
        xT = np.ascontiguousarray(np.concatenate([s0, s1], axis=0).T)
        cosT, sinT = rope_tables(rows, cols)
        v = vecs.copy()
        v[:, FP_COL] = fp
        v[:, FP_COL + 1] = 1.0 - fp
        mask = np.zeros((128, 67), np.float32)
        if core < 4:
            for half in range(2):
                mask[16:32, half * 33 + 32] = NEG
        else:
            mask[:, 16:32] = NEG
            mask[16:32, 32] = NEG
            mask[:, 33:33 + 16] = NEG
            mask[0:16, 33 + 32] = NEG
        m = {"xT": xT, "metaT": metaT, "vecs": v, "mask": mask, "cosT": cosT, "sinT": sinT,
             "perm": perm, "onesm": onesm, "bones": bones, "zeros": zeros}
        for n, _ in WSHAPES:
            m[n] = np.asarray(inputs[n], dtype=np.float32)
        in_maps.append(m)
    return in_maps


def kernel(**inputs):
    in_maps = _host_prep(inputs)
    nc = build_nc()
    res = run_bass_kernel_spmd(nc, in_maps, core_ids=list(range(8)))
    y_prompt = np.zeros((4, 4096, D), np.float32)
    y_sample = np.zeros((16, 2048, D), np.float32)
    for core in range(8):
        y = np.asarray(res.results[core]["yT"]).T
        if core < 4:
            y_prompt[core] = y[0:4096]
            y_sample[core] = y[4096:6144]
        else:
            j = core - 4
            y_sample[4 + 3 * j] = y[0:2048]
            y_sample[5 + 3 * j] = y[2048:4096]
            y_sample[6 + 3 * j] = y[4096:6144]
    return (y_prompt, y_sample)
```

```python
import numpy as np
from contextlib import ExitStack
import concourse.bass as bass
import concourse.mybir as mybir
from concourse.bass_utils import run_bass_kernel_spmd

F32 = mybir.dt.float32
BF16 = mybir.dt.bfloat16
AF = mybir.ActivationFunctionType
ALU = mybir.AluOpType

D = 1024
DFF = 2816
NL = 4
NREAL = 6144
NT = 6192
NTY = 6198
INW = 6656
NEG = -30000.0
VPL = 58
NV = NL * VPL + 8 + 4
GF_BASE = NL * VPL
FP_COL = NL * VPL + 8
EPS_COL = FP_COL + 2
ZERO_COL = FP_COL + 3
QCH = [(0, 6), (6, 6), (12, 5), (17, 5)]
NRING = 4

WSHAPES = [
    ("ffn1_w_gate", [NL, D, DFF]), ("ffn1_w_up", [NL, D, DFF]), ("ffn1_w_down", [NL, DFF, D]),
    ("w_in", [NL, D, INW]), ("w_o_attn", [NL, D, D]), ("w_o_conv", [NL, D, D]), ("w_merge", [NL, D, D]),
    ("ffn2_w_gate", [NL, D, DFF]), ("ffn2_w_up", [NL, D, DFF]), ("ffn2_w_down", [NL, DFF, D]),
]


def ycol(c):
    if c < 4096:
        return 17 + c
    if c < 6144:
        return c + 53
    if c < 6160:
        return 1 + (c - 6144)
    if c < 6176:
        return 4115 + (c - 6160)
    return 4133 + (c - 6176)


class Tracker:
    def __init__(self, nc, es):
        self.nc, self.es = nc, es
        self.E = {"pe": nc.tensor, "act": nc.scalar, "dve": nc.vector, "pool": nc.gpsimd, "sp": nc.sync}
        self.sems, self.cnt, self.waited = {}, {}, {}
        self.lw, self.rd = {}, {}
        self.nsem = 0
        for e in ("pe", "act", "dve", "pool"):
            self._sem(e)

    def _sem(self, name):
        if name not in self.sems:
            self.nsem += 1
            self.sems[name] = self.es.enter_context(self.nc.semaphore("sm%d" % self.nsem))
            self.cnt[name] = 0
        return self.sems[name]

    def _deps(self, eng, reads, writes):
        d = {}

        def add(x, raw):
            if x is None:
                return
            sem, val = x
            if sem == eng and (eng == "pe" or not raw):
                return
            if d.get(sem, 0) < val:
                d[sem] = val

        for k in reads:
            add(self.lw.get(k), True)
        for k in writes:
            add(self.lw.get(k), False)
            for r in self.rd.get(k, {}).items():
                add(r, False)
        return d

    def _need(self, eng, d):
        for sem, val in d.items():
            if self.waited.get((eng, sem), 0) < val:
                self.E[eng].wait_ge(self.sems[sem], val)
                self.waited[(eng, sem)] = val

    def _done(self, sem, val, reads, writes):
        for k in writes:
            self.lw[k] = (sem, val)
            self.rd[k] = {}
        for k in reads:
            r = self.rd.setdefault(k, {})
            if r.get(sem, 0) < val:
                r[sem] = val

    def op(self, eng, reads, writes, fn):
        self._need(eng, self._deps(eng, reads, writes))
        ins = fn()
        self.cnt[eng] += 1
        ins.then_inc(self.sems[eng], 1)
        self._done(eng, self.cnt[eng], reads, writes)

    def dma(self, q, semkey, reads, writes, out, in_, slow=False):
        self._sem(semkey)
        self._need(q, self._deps(q, reads, writes))
        if slow:
            ins = self.E[q].dma_start(out=out, in_=in_, allow_slow_non_contiguous=True)
        else:
            ins = self.E[q].dma_start(out=out, in_=in_)
        self.cnt[semkey] += 16
        ins.then_inc(self.sems[semkey], 16)
        self._done(semkey, self.cnt[semkey], reads, writes)

    def barrier(self):
        for e in ("pe", "act", "dve", "pool", "sp"):
            d = {s: c for s, c in self.cnt.items() if c > 0 and s != e}
            self._need(e, d)
        self.lw, self.rd = {}, {}


def build_nc(nlayers=NL, dbg=False):
    nc = bass.Bass("TRN2", target_bir_lowering=False)
    dbgT = nc.dram_tensor("dbgT", [D, 48], F32, kind="ExternalOutput").ap() if dbg else None
    dbgA = nc.dram_tensor("dbgA", [D, 48], F32, kind="ExternalOutput").ap() if dbg else None

    def din(name, shape, dt=F32):
        return nc.dram_tensor(name, shape, dt, kind="ExternalInput").ap()

    def dscr(name, shape, dt):
        return nc.dram_tensor(name, shape, dt, kind="Internal").ap()

    xT = din("xT", [D, NREAL])
    metaT = din("metaT", [D, 16])
    Wd = {n: din(n, shp) for n, shp in WSHAPES}
    vecs_d = din("vecs", [128, NV])
    mask_d = din("mask", [128, 67])
    cos_d = din("cosT", [128, NT])
    sin_d = din("sinT", [128, NT])
    perm_d = din("perm", [128, 128])
    ones_d = din("onesm", [128, 128])
    bones_d = din("bones", [128, 128])
    zeros_d = din("zeros", [128, 8])
    yT = nc.dram_tensor("yT", [D, NREAL], F32, kind="ExternalOutput").ap()
    hT = dscr("hT", [D, NT], F32)
    qT = dscr("qT", [D, NT], BF16)
    aT = dscr("aT", [D, NT], BF16)
    yS = dscr("yS", [2, D, NTY], BF16)
    cbT = dscr("cbT", [D, NT], BF16)
    sgaT = dscr("sgaT", [D, NT], BF16)
    sgcT = dscr("sgcT", [D, NT], BF16)

    def fm(ap):
        return ap.rearrange("(k p) n -> p k n", p=128)

    groups = []
    for g in range(6):
        tiles = [(0, 512), (512, 512)]
        if g == 5:
            tiles.append((1024, 48))
        groups.append(dict(g=g, G0=g * 1024, tiles=tiles, W=sum(w for _, w in tiles)))

    with ExitStack() as es:
        def sb(name, shape, dt):
            return es.enter_context(nc.sbuf_tensor(name, shape, dt))

        KT = sb("KT", [128, 2, NT], BF16)
        VX = sb("VX", [128, 50, 384], BF16)
        h = sb("h", [128, 8, 1072], F32)
        xn = sb("xn", [128, 8, 1072], BF16)
        actb = sb("actb", [128, 6, 1072], BF16)
        ring = sb("ring", [128, NRING, 4096], BF16)
        CS = sb("CS", [128, 2, 1072], F32)
        tmpf = sb("tmpf", [128, 10, 512], F32)
        stg = sb("stg", [128, 12, 514], BF16)
        vec = sb("vec", [128, NV], F32)
        msk = sb("msk", [128, 67], F32)
        perm = sb("perm_s", [128, 128], F32)
        ones = sb("ones_s", [128, 128], BF16)
        bones = sb("bones_s", [128, 128], BF16)
        hB = sb("hB", [128, 8], BF16)
        Cded = sb("Cded", [128, 8, 512], BF16)
        zt = sb("zt", [128, 8], BF16)
        ps = es.enter_context(nc.psum_tensor("ps", [128, 8, 512], F32))
        tr = Tracker(nc, es)
        st_ = {"bk": 0, "st": 0, "ring": 0, "sg": 0}

        def bk():
            b = st_["bk"]
            st_["bk"] = (b + 1) % 8
            return b

        def bk2():
            if st_["bk"] % 2:
                st_["bk"] = (st_["bk"] + 1) % 8
            b = st_["bk"]
            st_["bk"] = (b + 2) % 8
            return b

        def stn():
            s = st_["st"]
            st_["st"] = (s + 1) % 12
            return s

        def sgn():
            s = st_["sg"]
            st_["sg"] = (s + 1) % 3
            return 4 + s

        def ringv(s, nk, bw):
            return ring[:, s, 0:nk * bw].rearrange("p (k m) -> p k m", m=bw)

        def wload(src, nk, bw):
            s = st_["ring"]
            st_["ring"] = (s + 1) % NRING
            tr.dma("pool", ("ring", s), [], [("ring", s)], out=ringv(s, nk, bw), in_=src)
            return s

        mm = nc.tensor.matmul

        def pe_group(wfn, nk, sfn, tiles, banks, reads, split=False):
            if split and len(tiles) > 1:
                rest = [r for r in reads if not (isinstance(r, tuple) and r[0] == "xn")]
                for i, (c0, w) in enumerate(tiles):
                    pe_group(wfn, nk, sfn, [(c0, w)], [banks[i]], [xk(k, c0) for k in range(8)] + rest)
                return

            def fn():
                last = None
                for k in range(nk):
                    for i, (c0, w) in enumerate(tiles):
                        last = mm(ps[:, banks[i], 0:w], lhsT=wfn(k), rhs=sfn(k, c0, w), start=(k == 0), stop=(k == nk - 1))
                return last
            tr.op("pe", reads, [("ps", b) for b in banks], fn)

        tr.dma("sp", "c0", [], ["vec"], out=vec[:], in_=vecs_d)
        tr.dma("sp", "c0", [], ["msk"], out=msk[:], in_=mask_d)
        tr.dma("sp", "c0", [], ["perm"], out=perm[:], in_=perm_d)
        tr.dma("pool", "c1", [], ["ones"], out=ones[:], in_=ones_d)
        tr.dma("pool", "c1", [], ["bones"], out=bones[:], in_=bones_d)
        tr.dma("pool", "c1", [], ["zt"], out=zt[:], in_=zeros_d)
        tr.op("dve", [], ["VX"], lambda: nc.vector.memset(VX[:], 1.0))
        tr.barrier()
        for par in range(2):
            for gc in (0, 4113, 4114, 4131, 4132, 6197):
                tr.dma("sp", "c2", ["zt"], [("yS", "guard", par, gc)], out=fm(yS[par])[:, :, gc:gc + 1], in_=zt[:].rearrange("p (k o) -> p k o", o=1), slow=True)
        tr.barrier()

        xn_keys = [("xn", k, hf) for k in range(8) for hf in range(2)]

        def xk(k, c0):
            return ("xn", k, 0 if c0 < 512 else 1)
        h_keys = [("h", k) for k in range(8)]

        def norm(g, gvbase, dst_xn=True):
            for (c0, w) in g["tiles"]:
                b = bk()
                for k in range(8):
                    s = stn()
                    tr.op("act", [("h", k)], [("st", s)], lambda: nc.scalar.activation(out=stg[:, s, 0:w], in_=h[:, k, c0:c0 + w], func=AF.Square))
                    tr.op("pe", [("st", s), "ones"], [("ps", b)], lambda: mm(ps[:, b, 0:w], lhsT=ones[:], rhs=stg[:, s, 0:w], start=(k == 0), stop=(k == 7)))
                tr.op("act", [("ps", b), "vec"], [("tf", 0)], lambda: nc.scalar.activation(out=tmpf[:, 0, 0:w], in_=ps[:, b, 0:w], func=AF.Sqrt, bias=vec[:, EPS_COL:EPS_COL + 1], scale=1.0 / D))
                tr.op("dve", [("tf", 0)], [("tf", 0)], lambda: nc.vector.reciprocal(out=tmpf[:, 0, 0:w], in_=tmpf[:, 0, 0:w]))
                for k in range(8):
                    if dst_xn:
                        tr.op("dve", [("h", k), ("tf", 0), "vec"], [xk(k, c0)], lambda: nc.vector.scalar_tensor_tensor(
                            out=xn[:, k, c0:c0 + w], in0=h[:, k, c0:c0 + w], scalar=vec[:, gvbase + k:gvbase + k + 1], in1=tmpf[:, 0, 0:w], op0=ALU.mult, op1=ALU.mult))
                    else:
                        tr.op("dve", [("h", k), ("tf", 0), "vec"], [("h", k)], lambda: nc.vector.scalar_tensor_tensor(
                            out=h[:, k, c0:c0 + w], in0=h[:, k, c0:c0 + w], scalar=vec[:, gvbase + k:gvbase + k + 1], in1=tmpf[:, 0, 0:w], op0=ALU.mult, op1=ALU.mult))

        def ffn(g, l, wg, wu, wdn, drain_jobs=False):
            tiles = g["tiles"]
            wgf = Wd[wg][l].rearrange("(k p) m -> p k m", p=128)
            wuf = Wd[wu][l].rearrange("(k p) m -> p k m", p=128)
            wdf = Wd[wdn][l].rearrange("(f p) m -> p f m", p=128)
            for (f0, nf) in QCH:
                for (b0, nb) in ((0, 3), (3, nf - 3)):
                    bw = nb * 128
                    cs = (f0 + b0) * 128
                    s_g = wload(wgf[:, :, cs:cs + bw], 8, bw)
                    s_u = wload(wuf[:, :, cs:cs + bw], 8, bw)
                    for j in range(nb):
                        fs = b0 + j
                        gb = [bk() for _ in tiles]
                        ub = [bk() for _ in tiles]
                        first = (f0 == 0 and b0 == 0 and j == 0)
                        pe_group(lambda k: ringv(s_g, 8, bw)[:, k, j * 128:(j + 1) * 128], 8, lambda k, c0, w: xn[:, k, c0:c0 + w], tiles, gb, xn_keys + [("ring", s_g)], split=first)
                        pe_group(lambda k: ringv(s_u, 8, bw)[:, k, j * 128:(j + 1) * 128], 8, lambda k, c0, w: xn[:, k, c0:c0 + w], tiles, ub, xn_keys + [("ring", s_u)], split=first)
                        for i, (c0, w) in enumerate(tiles):
                            t = sgn()
                            tr.op("act", [("ps", gb[i])], [("tf", t)], lambda: nc.scalar.activation(out=tmpf[:, t, 0:w], in_=ps[:, gb[i], 0:w], func=AF.Silu))
                            tr.op("dve", [("ps", ub[i]), ("tf", t)], [("act", fs)], lambda: nc.vector.tensor_tensor(out=actb[:, fs, c0:c0 + w], in0=ps[:, ub[i], 0:w], in1=tmpf[:, t, 0:w], op=ALU.mult))
                        if drain_jobs:
                            drain(1)
                for mb in range(2):
                    s_d = wload(wdf[:, f0:f0 + nf, mb * 512:(mb + 1) * 512], nf, 512)
                    for j in range(4):
                        m = mb * 4 + j
                        ob = [bk() for _ in tiles]
                        pe_group(lambda fc: ringv(s_d, nf, 512)[:, fc, j * 128:(j + 1) * 128], nf, lambda fc, c0, w: actb[:, fc, c0:c0 + w], tiles, ob,
                                 [("act", fc) for fc in range(nf)] + [("ring", s_d)])
                        for i, (c0, w) in enumerate(tiles):
                            tr.op("dve", [("ps", ob[i]), ("h", m)], [("h", m)], lambda: nc.vector.scalar_tensor_tensor(
                                out=h[:, m, c0:c0 + w], in0=ps[:, ob[i], 0:w], scalar=0.5, in1=h[:, m, c0:c0 + w], op0=ALU.mult, op1=ALU.add))

        def store_cols(dst_fm, key, chunk, s, g, c0, w):
            col = g["G0"] + c0
            tr.dma("sp", ("st", s), [("st", s)], [(key, chunk, col)], out=dst_fm[:, chunk, col:col + w], in_=stg[:, s, 0:w])

        def qk_chunk(g, l, bs, tiles, gcol, is_q, chunk):
            n = len(tiles)
            T = [(1 + 3 * i, 2 + 3 * i, 3 + 3 * i) for i in range(n)]
            sq = []
            for i, (c0, w) in enumerate(tiles):
                t1 = T[i][0]
                b = bs[i]
                tr.op("act", [("ps", b)], [("tf", t1)], lambda: nc.scalar.activation(out=tmpf[:, t1, 0:w], in_=ps[:, b, 0:w], func=AF.Copy))
                s_ = stn()
                sq.append(s_)
                tr.op("dve", [("tf", t1)], [("st", s_)], lambda: nc.vector.tensor_tensor(out=stg[:, s_, 0:w], in0=tmpf[:, t1, 0:w], in1=tmpf[:, t1, 0:w], op=ALU.mult))
            yield
            b2s = []
            for i, (c0, w) in enumerate(tiles):
                b2 = bk()
                b2s.append(b2)
                s_ = sq[i]
                tr.op("pe", [("st", s_), "bones"], [("ps", b2)], lambda: mm(ps[:, b2, 0:w], lhsT=bones[:], rhs=stg[:, s_, 0:w], start=True, stop=True))
            for i, (c0, w) in enumerate(tiles):
                t1, t2, t3 = T[i]
                b2 = b2s[i]
                tr.op("act", [("ps", b2), "vec"], [("tf", t2)], lambda: nc.scalar.activation(out=tmpf[:, t2, 0:w], in_=ps[:, b2, 0:w], func=AF.Sqrt, bias=vec[:, EPS_COL:EPS_COL + 1], scale=1.0 / 64))
                tr.op("dve", [("tf", t2)], [("tf", t2)], lambda: nc.vector.reciprocal(out=tmpf[:, t2, 0:w], in_=tmpf[:, t2, 0:w]))
                tr.op("dve", [("tf", t1), ("tf", t2), "vec"], [("tf", t1)], lambda: nc.vector.scalar_tensor_tensor(
                    out=tmpf[:, t1, 0:w], in0=tmpf[:, t1, 0:w], scalar=vec[:, gcol:gcol + 1], in1=tmpf[:, t2, 0:w], op0=ALU.mult, op1=ALU.mult))
            yield
            b3s = []
            for i, (c0, w) in enumerate(tiles):
                t1 = T[i][0]
                b3 = bk()
                b3s.append(b3)
                tr.op("pe", [("tf", t1), "perm"], [("ps", b3)], lambda: mm(ps[:, b3, 0:w], lhsT=perm[:], rhs=tmpf[:, t1, 0:w], start=True, stop=True))
            for i, (c0, w) in enumerate(tiles):
                t1, t2, t3 = T[i]
                b3 = b3s[i]
                tr.op("dve", [("tf", t1), "CS"], [("tf", t3)], lambda: nc.vector.tensor_tensor(out=tmpf[:, t3, 0:w], in0=tmpf[:, t1, 0:w], in1=CS[:, 0, c0:c0 + w], op=ALU.mult))
                tr.op("dve", [("ps", b3), "CS"], [("tf", t2)], lambda: nc.vector.tensor_tensor(out=tmpf[:, t2, 0:w], in0=ps[:, b3, 0:w], in1=CS[:, 1, c0:c0 + w], op=ALU.mult))
                if is_q:
                    s2 = stn()
                    tr.op("dve", [("tf", t2), ("tf", t3)], [("st", s2)], lambda: nc.vector.tensor_tensor(out=stg[:, s2, 0:w], in0=tmpf[:, t3, 0:w], in1=tmpf[:, t2, 0:w], op=ALU.add))
                    store_cols(fm(qT), "qT", chunk, s2, g, c0, w)
                else:
                    col = g["G0"] + c0
                    tr.op("dve", [("tf", t2), ("tf", t3)], [("KT", chunk)], lambda: nc.vector.tensor_tensor(out=KT[:, chunk, col:col + w], in0=tmpf[:, t3, 0:w], in1=tmpf[:, t2, 0:w], op=ALU.add))
            yield

        def w_in_pass(g, l):
            tiles = g["tiles"]
            G0, Wg = g["G0"], g["W"]
            win = Wd["w_in"][l]
            ysf = fm(yS[l % 2])
            winf = win.rearrange("(k p) m -> p k m", p=128)
            vb = l * VPL
            tr.dma("sp", "cs", [], ["CS"], out=CS[:, 0, 0:Wg], in_=cos_d[:, G0:G0 + Wg])
            tr.dma("sp", "cs", [], ["CS"], out=CS[:, 1, 0:Wg], in_=sin_d[:, G0:G0 + Wg])

            nmain = [0]

            def main(s, j):
                bs = [bk() for _ in tiles]
                pe_group(lambda k: ringv(s, 8, 512)[:, k, j * 128:(j + 1) * 128], 8, lambda k, c0, w: xn[:, k, c0:c0 + w], tiles, bs, xn_keys + [("ring", s)], split=(nmain[0] == 0))
                nmain[0] += 1
                return bs

            def load_q(qb):
                s = st_["ring"]
                st_["ring"] = (s + 1) % NRING
                sv = ringv(s, 8, 512).rearrange("p k (c hf j) -> p k c hf j", c=4, hf=2, j=64)
                src = winf[:, :, qb * 512:(qb + 1) * 512].rearrange("p k (hf c j) -> p k c hf j", hf=2, c=4, j=64)
                for hf in range(2):
                    for cl in range(4):
                        tr.dma("pool", ("ring", s), [], [("ring", s)], out=sv[:, :, cl, hf, :], in_=src[:, :, cl, hf, :])
                return s

            def evac_simple(bs, func, dstT, key, chunk):
                for i, (c0, w) in enumerate(tiles):
                    s2 = stn()
                    tr.op("act", [("ps", bs[i])], [("st", s2)], lambda: nc.scalar.activation(out=stg[:, s2, 0:w], in_=ps[:, bs[i], 0:w], func=func))
                    store_cols(fm(dstT), key, chunk, s2, g, c0, w)

            for qb in range(2):
                s_q = load_q(qb)
                s_cb = wload(winf[:, :, 1536 + qb * 512:1536 + (qb + 1) * 512], 8, 512)
                s_ga = wload(winf[:, :, 4608 + qb * 512:4608 + (qb + 1) * 512], 8, 512)
                for j in range(4):
                    chunk = qb * 4 + j
                    bs = main(s_q, j)
                    st1 = qk_chunk(g, l, bs, tiles, vb + 56, True, chunk)
                    next(st1)
                    b1 = main(s_cb, j)
                    evac_simple(b1, AF.Copy, cbT, "cbT", chunk)
                    next(st1)
                    b2 = main(s_ga, j)
                    evac_simple(b2, AF.Sigmoid, sgaT, "sgaT", chunk)
                    next(st1)
            s = wload(winf[:, :, 1024:1536], 8, 512)
            s_gc = wload(winf[:, :, 5632:5632 + 512], 8, 512)
            for j in range(2):
                bs = main(s, j)
                st1 = qk_chunk(g, l, bs, tiles, vb + 57, False, j)
                next(st1)
                b1 = main(s_gc, 2 * j)
                evac_simple(b1, AF.Sigmoid, sgcT, "sgcT", 2 * j)
                next(st1)
                b2 = main(s_gc, 2 * j + 1)
                evac_simple(b2, AF.Sigmoid, sgcT, "sgcT", 2 * j + 1)
                next(st1)
            vblocks = [(j * 128, 128, (G0 + j * 128) // 128) for j in range(8)]
            if g["g"] == 5:
                vblocks += [(1024, 32, 48), (1056, 16, 49)]
            for (c0, nr, kc) in vblocks:
                b = bk()

                def fnv():
                    last = None
                    for k in range(8):
                        last = mm(ps[0:nr, b, 0:256], lhsT=xn[:, k, c0:c0 + nr], rhs=ringv(s, 8, 512)[:, k, 256:512], start=(k == 0), stop=(k == 7))
                    return last
                tr.op("pe", xn_keys + [("ring", s)], [("ps", b)], fnv)

                def fnc():
                    last = None
                    for hd in range(4):
                        dst = (hd // 2) * 192 + (hd % 2) * 128
                        last = nc.vector.tensor_copy(out=VX[0:nr, kc, dst:dst + 64], in_=ps[0:nr, b, hd * 64:(hd + 1) * 64])
                    return last
                tr.op("dve", [("ps", b)], [("VX", kc)], fnc)
            s_gc = wload(winf[:, :, 5632 + 512:5632 + 1024], 8, 512)
            for j in range(4):
                b1 = main(s_gc, j)
                evac_simple(b1, AF.Sigmoid, sgcT, "sgcT", 4 + j)
            for blk in range(2):
                s_c = wload(winf[:, :, 2560 + blk * 512:2560 + (blk + 1) * 512], 8, 512)
                s_x = wload(winf[:, :, 3584 + blk * 512:3584 + (blk + 1) * 512], 8, 512)
                for j in range(4):
                    chunk = blk * 4 + j
                    bc = main(s_c, j)
                    bx = main(s_x, j)
                    for i, (c0, w) in enumerate(tiles):
                        t = sgn()
                        tr.op("act", [("ps", bc[i])], [("tf", t)], lambda: nc.scalar.activation(out=tmpf[:, t, 0:w], in_=ps[:, bc[i], 0:w], func=AF.Copy))
                        s2 = stn()
                        tr.op("dve", [("ps", bx[i]), ("tf", t)], [("st", s2)], lambda: nc.vector.tensor_tensor(out=stg[:, s2, 0:w], in0=ps[:, bx[i], 0:w], in1=tmpf[:, t, 0:w], op=ALU.mult))
                        col = G0 + c0
                        if w == 48:
                            for sgm in range(3):
                                yc = ycol(col + sgm * 16)
                                tr.dma("sp", ("st", s2), [("st", s2)], [("yS", chunk, "m", sgm)], out=ysf[:, chunk, yc:yc + 16], in_=stg[:, s2, sgm * 16:(sgm + 1) * 16])
                        else:
                            yc = ycol(col)
                            tr.dma("sp", ("st", s2), [("st", s2)], [("yS", chunk, col)], out=ysf[:, chunk, yc:yc + w], in_=stg[:, s2, 0:w])
                            if col == 2048:
                                tr.dma("sp", ("st", s2), [("st", s2)], [("yS", chunk, "b15r")], out=ysf[:, chunk, 4131:4132], in_=stg[:, s2, 0:1], slow=True)

        A_t = xn[:, :, 0:512]
        C_xn = xn[:, :, 536:1048]
        M_t = actb[:, :, :].rearrange("p f n -> p (f n)")[:, 0:4096].rearrange("p (k n) -> p k n", n=512)
        a_keys = [("xn", k, 0) for k in range(8)]
        m_keys = [("act", f) for f in range(6)]
        pending = []

        def cbuf(ti):
            if ti == 1:
                return C_xn, [("xn", k, 1) for k in range(8)]
            return Cded, ["Cd"]

        def conv_jobs(g, l, ti):
            G0 = g["G0"]
            vb = l * VPL
            cbf = fm(cbT)
            ysf = fm(yS[l % 2])
            (c0, w) = g["tiles"][ti]
            col = G0 + c0
            Cb, ckeys = cbuf(ti)
            if w == 48:
                segs = [(0, 16, 0), (16, 16, 4114), (32, 16, 4132)]
            else:
                segs = [(0, w, ycol(col) - 1)]

            def job(c):
                if c == 0 and col == 2048:
                    tr.dma("sp", "hb", [("yS", "any")], ["hB"], out=hB[:].rearrange("p (k o) -> p k o", o=1), in_=ysf[:, :, 4130:4131], slow=True)
                for (o0, sw, y0) in segs:
                    sy = stn()
                    tr.dma("sp", ("st", sy), [("yS", "any")], [("st", sy)], out=stg[:, sy, 0:sw + 2], in_=ysf[:, c, y0:y0 + sw + 2])
                    sc = stn()
                    tr.dma("sp", ("st", sc), [("cbT", "any")], [("st", sc)], out=stg[:, sc, 0:sw], in_=cbf[:, c, col + o0:col + o0 + sw])
                    if col == 1536:
                        tr.op("dve", [("st", sy), "vec"], [("st", sy)], lambda: nc.vector.tensor_scalar(
                            out=stg[:, sy, sw + 1:sw + 2], in0=stg[:, sy, sw + 1:sw + 2], scalar1=vec[:, FP_COL:FP_COL + 1], scalar2=None, op0=ALU.mult))
                    if col == 2048:
                        tr.op("dve", [("st", sy), "vec"], [("st", sy)], lambda: nc.vector.tensor_scalar(
                            out=stg[:, sy, 0:1], in0=stg[:, sy, 0:1], scalar1=vec[:, FP_COL:FP_COL + 1], scalar2=None, op0=ALU.mult))
                        tr.op("dve", [("st", sy), "vec", "hB"], [("st", sy)], lambda: nc.vector.scalar_tensor_tensor(
                            out=stg[:, sy, 0:1], in0=hB[:, c:c + 1], scalar=vec[:, FP_COL + 1:FP_COL + 2], in1=stg[:, sy, 0:1], op0=ALU.mult, op1=ALU.add))
                    tr.op("dve", [("st", sy), "vec"], [("tf", 3)], lambda: nc.vector.tensor_scalar(
                        out=tmpf[:, 3, 0:sw], in0=stg[:, sy, 0:sw], scalar1=vec[:, vb + 24 + c:vb + 25 + c], scalar2=None, op0=ALU.mult))
                    tr.op("dve", [("st", sy), "vec", ("tf", 3)], [("tf", 3)], lambda: nc.vector.scalar_tensor_tensor(
                        out=tmpf[:, 3, 0:sw], in0=stg[:, sy, 1:sw + 1], scalar=vec[:, vb + 32 + c:vb + 33 + c], in1=tmpf[:, 3, 0:sw], op0=ALU.mult, op1=ALU.add))
                    tr.op("dve", [("st", sy), "vec", ("tf", 3)], [("tf", 3)], lambda: nc.vector.scalar_tensor_tensor(
                        out=tmpf[:, 3, 0:sw], in0=stg[:, sy, 2:sw + 2], scalar=vec[:, vb + 40 + c:vb + 41 + c], in1=tmpf[:, 3, 0:sw], op0=ALU.mult, op1=ALU.add))
                    tr.op("dve", [("st", sc), "vec", ("tf", 3)], ckeys, lambda: nc.vector.scalar_tensor_tensor(
                        out=Cb[:, c, o0:o0 + sw], in0=tmpf[:, 3, 0:sw], scalar=vec[:, vb + 48 + c:vb + 49 + c], in1=stg[:, sc, 0:sw], op0=ALU.add, op1=ALU.mult))
            return [(lambda c=c: job(c)) for c in range(8)]

        def drain(n=None):
            k = 0
            while pending and (n is None or k < n):
                pending.pop(0)()
                k += 1

        def proj_tile(g, l, ti):
            G0 = g["G0"]
            (c0, w) = g["tiles"][ti]
            col = G0 + c0
            aTf, sgaf, sgcf = fm(aT), fm(sgaT), fm(sgcT)
            woa = Wd["w_o_attn"][l]
            wocf = Wd["w_o_conv"][l].rearrange("(k p) m -> p k m", p=128)
            wmf = Wd["w_merge"][l].rearrange("(k p) m -> p k m", p=128)
            Cb, ckeys = cbuf(ti)
            tr.dma("sp", "aload", [("aT", col)], a_keys, out=A_t[:, :, 0:w], in_=aTf[:, :, col:col + w])
            for blk in range(2):
                s_a = st_["ring"]
                st_["ring"] = (s_a + 1) % NRING
                for hf in range(2):
                    for cb_ in range(2):
                        src = woa[:, blk * 512:(blk + 1) * 512].rearrange("(cb hf cl r) m -> cb hf r cl m", cb=2, hf=2, cl=4, r=64)[cb_, hf]
                        dst = ringv(s_a, 8, 512)[hf * 64:(hf + 1) * 64, cb_ * 4:(cb_ + 1) * 4, :]
                        tr.dma("pool", ("ring", s_a), [], [("ring", s_a)], out=dst, in_=src)
                s_c = wload(wocf[:, :, blk * 512:(blk + 1) * 512], 8, 512)
                for j in range(4):
                    m = blk * 4 + j
                    ba, bc = bk(), bk()
                    pe_group(lambda k: ringv(s_a, 8, 512)[:, k, j * 128:(j + 1) * 128], 8, lambda k, cc0, ww: A_t[:, k, 0:ww], [(0, w)], [ba], a_keys + [("ring", s_a)])
                    pe_group(lambda k: ringv(s_c, 8, 512)[:, k, j * 128:(j + 1) * 128], 8, lambda k, cc0, ww: Cb[:, k, 0:ww], [(0, w)], [bc], ckeys + [("ring", s_c)])
                    sa, sc2 = stn(), stn()
                    tr.dma("sp", ("st", sa), [("sgaT", "any")], [("st", sa)], out=stg[:, sa, 0:w], in_=sgaf[:, m, col:col + w])
                    tr.dma("sp", ("st", sc2), [("sgcT", "any")], [("st", sc2)], out=stg[:, sc2, 0:w], in_=sgcf[:, m, col:col + w])
                    tr.op("dve", [("ps", ba), ("st", sa)], [("tf", 1)], lambda: nc.vector.tensor_tensor(out=tmpf[:, 1, 0:w], in0=ps[:, ba, 0:w], in1=stg[:, sa, 0:w], op=ALU.mult))
                    tr.op("dve", [("ps", bc), ("st", sc2)], [("tf", 2)], lambda: nc.vector.tensor_tensor(out=tmpf[:, 2, 0:w], in0=ps[:, bc, 0:w], in1=stg[:, sc2, 0:w], op=ALU.mult))
                    tr.op("dve", [("tf", 1), ("tf", 2)], m_keys, lambda: nc.vector.tensor_tensor(out=M_t[:, m, 0:w], in0=tmpf[:, 1, 0:w], in1=tmpf[:, 2, 0:w], op=ALU.add))
                    drain(1)
            for blk in range(2):
                s_m = wload(wmf[:, :, blk * 512:(blk + 1) * 512], 8, 512)
                for j in range(4):
                    m = blk * 4 + j
                    bo = bk()
                    pe_group(lambda k: ringv(s_m, 8, 512)[:, k, j * 128:(j + 1) * 128], 8, lambda k, cc0, ww: M_t[:, k, 0:ww], [(0, w)], [bo], m_keys + [("ring", s_m)])
                    tr.op("dve", [("ps", bo), ("h", m)], [("h", m)], lambda: nc.vector.tensor_tensor(out=h[:, m, c0:c0 + w], in0=ps[:, bo, 0:w], in1=h[:, m, c0:c0 + w], op=ALU.add))

        def p2_front(g, l, prefetched):
            nt = len(g["tiles"])
            if not prefetched:
                pending.extend(conv_jobs(g, l, 0))
            drain()
            for ti in range(nt):
                if ti + 1 < nt:
                    pending.extend(conv_jobs(g, l, ti + 1))
                proj_tile(g, l, ti)
                drain()

        def attention(l):
            qTf, aTf = fm(qT), fm(aT)
            Pb = [actb[:, i, 0:1024] for i in range(2)]
            Ao = actb[:, :, :].rearrange("p f n -> p (f n)")[:, 2144:2144 + 4096].rearrange("p (k n) -> p k n", n=512)
            Qb = [xn[:, :, 0:512], xn[:, :, 536:1048]]
            slots = []
            kch0 = [(j * 128, 128, j, None) for j in range(32)] + [(6144, 32, 48, None)]
            qt0 = [(j * 512, 512, j // 4) for j in range(8)] + [(6144, 16, 0), (6160, 16, 1)]
            kch1 = [(4096 + j * 128, 128, 32 + j, None) for j in range(16)] + [(6176, 16, 49, None)]
            qt1 = [(4096 + j * 512, 512, -1) for j in range(4)] + [(6176, 16, -1)]
            xyc = [0]
            items = [(qc, w, half, kch0) for (qc, w, half) in qt0] + [(qc, w, half, kch1) for (qc, w, half) in qt1]

            def qload(n):
                (qc_, w_, _, _) = items[n]
                tr.dma("sp", ("Q", n % 2), [("qT", "any")], [("Q", n % 2)], out=Qb[n % 2][:, :, 0:w_], in_=qTf[:, :, qc_:qc_ + w_])
            qload(0)
            for qi, (qc, w, half, kchs) in enumerate(items):
                if True:
                    Q = Qb[qi % 2]
                    qkey = ("Q", qi % 2)
                    if qi + 1 < len(items):
                        qload(qi + 1)
                    for c in range(8):
                        kchunk = c // 4
                        pair = c // 4
                        bX, bY = (4, 5) if (xyc[0] % 2 == 0) else (6, 7)
                        xyc[0] += 1
                        nk = len(kchs)
                        state = {}

                        def s_step(i):
                            (kc0, nr, vxc, _) = kchs[i]
                            b2 = 2 * (i % 2)
                            state[i] = b2

                            def fn():
                                mm(ps[0:nr, b2, 0:w], lhsT=KT[0:64, kchunk, kc0:kc0 + nr], rhs=Q[0:64, c, 0:w], start=True, stop=True)
                                return mm(ps[0:nr, b2 + 1, 0:w], lhsT=KT[64:128, kchunk, kc0:kc0 + nr], rhs=Q[64:128, c, 0:w], start=True, stop=True)
                            tr.op("pe", [qkey, ("KT", kchunk)], [("ps", b2), ("ps", b2 + 1)], fn)

                        def e_step(i):
                            (kc0, nr, vxc, _) = kchs[i]
                            b2 = state[i]
                            P = Pb[i % 2]
                            pk = ("P", i % 2)
                            mc = ZERO_COL_M if half < 0 else half * 33 + i
                            tr.op("act", [("ps", b2), ("ps", b2 + 1), "msk"], [pk], lambda: nc.scalar.activation(
                                out=P[0:nr, :].rearrange("p (t n) -> p t n", t=2)[:, :, 0:w], in_=ps[0:nr, b2:b2 + 2, 0:w], func=AF.Exp, bias=msk[0:nr, mc:mc + 1], scale=0.125))

                        def v_step(i):
                            (kc0, nr, vxc, _) = kchs[i]
                            P = Pb[i % 2]
                            pk = ("P", i % 2)

                            def fn():
                                mm(ps[:, bX, 0:w], lhsT=VX[0:nr, vxc, pair * 192:pair * 192 + 128], rhs=P[0:nr, 0:w], start=(i == 0), stop=(i == nk - 1))
                                return mm(ps[:, bY, 0:w], lhsT=VX[0:nr, vxc, pair * 192 + 64:pair * 192 + 192], rhs=P[0:nr, 512:512 + w], start=(i == 0), stop=(i == nk - 1))
                            tr.op("pe", [pk, ("VX", vxc)], [("ps", bX), ("ps", bY)], fn)

                        s_step(0)
                        if nk > 1:
                            s_step(1)
                        for i in range(nk):
                            e_step(i)
                            if i + 2 < nk:
                                s_step(i + 2)
                            v_step(i)
                        tr.op("dve", [("ps", bX)], [("tf", 1)], lambda: nc.vector.reciprocal(out=tmpf[64:128, 1, 0:w], in_=ps[64:128, bX, 0:w]))
                        tr.op("dve", [("ps", bY)], [("tf", 2)], lambda: nc.vector.reciprocal(out=tmpf[0:64, 2, 0:w], in_=ps[0:64, bY, 0:w]))
                        tr.op("dve", [("ps", bX), ("tf", 1)], [("Ao", c, 0)], lambda: nc.vector.tensor_tensor(out=Ao[0:64, c, 0:w], in0=ps[0:64, bX, 0:w], in1=tmpf[64:128, 1, 0:w], op=ALU.mult))
                        tr.op("dve", [("ps", bY), ("tf", 2)], [("Ao", c, 1)], lambda: nc.vector.tensor_tensor(out=Ao[64:128, c, 0:w], in0=ps[64:128, bY, 0:w], in1=tmpf[0:64, 2, 0:w], op=ALU.mult))
                    tr.dma("sp", "astore", [("Ao", c_, hf_) for c_ in range(8) for hf_ in range(2)], [("aT", qc)], out=aTf[:, :, qc:qc + w], in_=Ao[:, :, 0:w])

        ZERO_COL_M = 66

        hTf = fm(hT)
        xTf = fm(xT)
        mTf = fm(metaT)
        yTf = fm(yT)
        for l in range(nlayers + 1):
            for g in groups:
                G0, Wg = g["G0"], g["W"]
                if l == 0:
                    tr.dma("act", "hload", [], h_keys, out=h[:, :, 0:1024], in_=xTf[:, :, G0:G0 + 1024])
                    if g["g"] == 5:
                        for sgm in range(3):
                            tr.dma("act", "hload", [], h_keys, out=h[:, :, 1024 + sgm * 16:1040 + sgm * 16], in_=mTf)
                else:
                    tr.dma("act", "hload", [("hT", g["g"])], h_keys, out=h[:, :, 0:Wg], in_=hTf[:, :, G0:G0 + Wg])
                    p2_front(g, l - 1, prefetched=(g["g"] > 0))
                    norm(g, (l - 1) * VPL + 16)
                    if g["g"] + 1 < len(groups):
                        pending.extend(conv_jobs(groups[g["g"] + 1], l - 1, 0))
                    ffn(g, l - 1, "ffn2_w_gate", "ffn2_w_up", "ffn2_w_down", drain_jobs=True)
                    drain()
                if l < nlayers:
                    norm(g, l * VPL + 0)
                    ffn(g, l, "ffn1_w_gate", "ffn1_w_up", "ffn1_w_down")
                    tr.dma("sp", "hstore", h_keys, [("hT", g["g"])], out=hTf[:, :, G0:G0 + Wg], in_=h[:, :, 0:Wg])
                    norm(g, l * VPL + 8)
                    w_in_pass(g, l)
                else:
                    norm(g, GF_BASE, dst_xn=False)
                    tr.dma("sp", "ostore", h_keys, [("yT", g["g"])], out=yTf[:, :, G0:G0 + 1024], in_=h[:, :, 0:1024])
                    if dbg and g["g"] == 5:
                        tr.dma("sp", "ostore", h_keys, [("dbg", 0)], out=fm(dbgT), in_=h[:, :, 1024:1072])
            if l < nlayers:
                tr.barrier()
                attention(l)
                tr.barrier()
        tr.barrier()
        if dbg:
            tr.dma("sp", "dbg1", [], [("xn", 0)], out=xn[:, :, 0:48], in_=fm(aT)[:, :, 6144:6192])
            tr.op("dve", [("xn", 0)], [("h", 0)], lambda: nc.vector.tensor_copy(out=h[:, :, 0:48], in_=xn[:, :, 0:48]))
            tr.dma("sp", "dbg1", [("h", 0)], [("dbgA", 0)], out=fm(dbgA), in_=h[:, :, 0:48])
            tr.barrier()
    return nc


def _host_prep(inputs):
    xp = np.asarray(inputs["x_prompt"], dtype=np.float32)
    xs = np.asarray(inputs["x_sample"], dtype=np.float32)
    meta = np.asarray(inputs["meta_tokens"], dtype=np.float32)
    metaT = np.ascontiguousarray(meta.T)
    vecs = np.zeros((128, NV), np.float32)

    def fmv(v):
        return np.asarray(v, np.float32).reshape(8, 128).T

    for l in range(NL):
        b = l * VPL
        vecs[:, b + 0:b + 8] = fmv(inputs["ffn1_norm"][l])
        vecs[:, b + 8:b + 16] = fmv(inputs["mix_norm"][l])
        vecs[:, b + 16:b + 24] = fmv(inputs["ffn2_norm"][l])
        vecs[:, b + 24:b + 32] = fmv(inputs["conv_w"][l][0])
        vecs[:, b + 32:b + 40] = fmv(inputs["conv_w"][l][1])
        vecs[:, b + 40:b + 48] = fmv(inputs["conv_w"][l][2])
        vecs[:, b + 48:b + 56] = fmv(inputs["conv_b"][l])
        vecs[:, b + 56] = np.tile(np.asarray(inputs["q_norm"][l], np.float32), 2)
        vecs[:, b + 57] = np.tile(np.asarray(inputs["k_norm"][l], np.float32), 2)
    vecs[:, GF_BASE:GF_BASE + 8] = fmv(inputs["final_norm"])
    vecs[:, EPS_COL] = 1e-6
    vecs[:, ZERO_COL] = 0.0
    perm = np.zeros((128, 128), np.float32)
    for m in range(128):
        partner = m + 16 if (m % 32) < 16 else m - 16
        perm[partner, m] = 1.0
    onesm = np.ones((128, 128), np.float32)
    bones = np.zeros((128, 128), np.float32)
    bones[:64, :64] = 1.0
    bones[64:, 64:] = 1.0
    zeros = np.zeros((128, 8), np.float32)
    inv_freq = (10000.0 ** (-np.arange(0, 32, 2, dtype=np.float32) / 32.0)).astype(np.float32)

    def rope_tables(rows, cols):
        p = np.arange(128)
        d = p % 64
        fi = d % 16
        isrow = d < 32
        neg = (d % 32) < 16
        pos = np.where(isrow[:, None], rows[None, :], cols[None, :]).astype(np.float32)
        ang = (pos * inv_freq[fi][:, None]).astype(np.float32)
        c = np.cos(ang).astype(np.float32)
        s = np.sin(ang).astype(np.float32)
        s = np.where(neg[:, None], -s, s).astype(np.float32)
        return np.ascontiguousarray(c), np.ascontiguousarray(s)

    tok4096 = np.arange(4096)
    tok2048 = np.arange(2048)
    mrow = np.full(16, -1.0, np.float32)
    mcol = np.arange(16, dtype=np.float32)
    in_maps = []
    for core in range(8):
        if core < 4:
            s0 = xp[core]
            s1 = xs[core]
            rows = np.concatenate([tok4096 // 64, tok2048 // 64, mrow, mrow, mrow]).astype(np.float32)
            cols = np.concatenate([tok4096 % 64, tok2048 % 64, mcol, mcol, mcol]).astype(np.float32)
            fp = 1.0
        else:
            j = core - 4
            s0 = np.concatenate([xs[4 + 3 * j], xs[5 + 3 * j]], axis=0)
            s1 = xs[6 + 3 * j]
            rows = np.concatenate([tok2048 // 64, tok2048 // 64, tok2048 // 64, mrow, mrow, mrow]).astype(np.float32)
            cols = np.concatenate([tok2048 % 64, tok2048 % 64, tok2048 % 64, mcol, mcol, mcol]).astype(np.float32)
            fp = 0.0
        xT = np.ascontiguousarray(np.concatenate([s0, s1], axis=0).T)
        cosT, sinT = rope_tables(rows, cols)
        v = vecs.copy()
        v[:, FP_COL] = fp
        v[:, FP_COL + 1] = 1.0 - fp
        mask = np.zeros((128, 67), np.float32)
        if core < 4:
            for half in range(2):
                mask[16:32, half * 33 + 32] = NEG
        else:
            mask[:, 16:32] = NEG
            mask[16:32, 32] = NEG
            mask[:, 33:33 + 16] = NEG
            mask[0:16, 33 + 32] = NEG
        m = {"xT": xT, "metaT": metaT, "vecs": v, "mask": mask, "cosT": cosT, "sinT": sinT,
             "perm": perm, "onesm": onesm, "bones": bones, "zeros": zeros}
        for n, _ in WSHAPES:
            m[n] = np.asarray(inputs[n], dtype=np.float32)
        in_maps.append(m)
    return in_maps


def kernel(**inputs):
    in_maps = _host_prep(inputs)
    nc = build_nc()
    res = run_bass_kernel_spmd(nc, in_maps, core_ids=list(range(8)))
    y_prompt = np.zeros((4, 4096, D), np.float32)
    y_sample = np.zeros((16, 2048, D), np.float32)
    for core in range(8):
        y = np.asarray(res.results[core]["yT"]).T
        if core < 4:
            y_prompt[core] = y[0:4096]
            y_sample[core] = y[4096:6144]
        else:
            j = core - 4
            y_sample[4 + 3 * j] = y[0:2048]
            y_sample[5 + 3 * j] = y[2048:4096]
            y_sample[6 + 3 * j] = y[4096:6144]
    return (y_prompt, y_sample)
```

```python
import numpy as np
from contextlib import ExitStack
import concourse.bass as bass
import concourse.mybir as mybir
from concourse.bass_utils import run_bass_kernel_spmd

F32 = mybir.dt.float32
BF16 = mybir.dt.bfloat16
AF = mybir.ActivationFunctionType
ALU = mybir.AluOpType

D = 1024
DFF = 2816
NL = 4
NREAL = 6144
NT = 6192
NTY = 6198
INW = 6656
NEG = -30000.0
VPL = 58
NV = NL * VPL + 8 + 4
GF_BASE = NL * VPL
FP_COL = NL * VPL + 8
EPS_COL = FP_COL + 2
ZERO_COL = FP_COL + 3
QCH = [(0, 6), (6, 6), (12, 5), (17, 5)]
NRING = 4

WSHAPES = [
    ("ffn1_w_gate", [NL, D, DFF]), ("ffn1_w_up", [NL, D, DFF]), ("ffn1_w_down", [NL, DFF, D]),
    ("w_in", [NL, D, INW]), ("w_o_attn", [NL, D, D]), ("w_o_conv", [NL, D, D]), ("w_merge", [NL, D, D]),
    ("ffn2_w_gate", [NL, D, DFF]), ("ffn2_w_up", [NL, D, DFF]), ("ffn2_w_down", [NL, DFF, D]),
]


def ycol(c):
    if c < 4096:
        return 17 + c
    if c < 6144:
        return c + 53
    if c < 6160:
        return 1 + (c - 6144)
    if c < 6176:
        return 4115 + (c - 6160)
    return 4133 + (c - 6176)


class Tracker:
    def __init__(self, nc, es):
        self.nc, self.es = nc, es
        self.E = {"pe": nc.tensor, "act": nc.scalar, "dve": nc.vector, "pool": nc.gpsimd, "sp": nc.sync}
        self.sems, self.cnt, self.waited = {}, {}, {}
        self.lw, self.rd = {}, {}
        self.nsem = 0
        for e in ("pe", "act", "dve", "pool"):
            self._sem(e)

    def _sem(self, name):
        if name not in self.sems:
            self.nsem += 1
            self.sems[name] = self.es.enter_context(self.nc.semaphore("sm%d" % self.nsem))
            self.cnt[name] = 0
        return self.sems[name]

    def _deps(self, eng, reads, writes):
        d = {}

        def add(x, raw):
            if x is None:
                return
            sem, val = x
            if sem == eng and (eng == "pe" or not raw):
                return
            if d.get(sem, 0) < val:
                d[sem] = val

        for k in reads:
            add(self.lw.get(k), True)
        for k in writes:
            add(self.lw.get(k), False)
            for r in self.rd.get(k, {}).items():
                add(r, False)
        return d

    def _need(self, eng, d):
        for sem, val in d.items():
            if self.waited.get((eng, sem), 0) < val:
                self.E[eng].wait_ge(self.sems[sem], val)
                self.waited[(eng, sem)] = val

    def _done(self, sem, val, reads, writes):
        for k in writes:
            self.lw[k] = (sem, val)
            self.rd[k] = {}
        for k in reads:
            r = self.rd.setdefault(k, {})
            if r.get(sem, 0) < val:
                r[sem] = val

    def op(self, eng, reads, writes, fn):
        self._need(eng, self._deps(eng, reads, writes))
        ins = fn()
        self.cnt[eng] += 1
        ins.then_inc(self.sems[eng], 1)
        self._done(eng, self.cnt[eng], reads, writes)

    def dma(self, q, semkey, reads, writes, out, in_, slow=False):
        self._sem(semkey)
        self._need(q, self._deps(q, reads, writes))
        if slow:
            ins = self.E[q].dma_start(out=out, in_=in_, allow_slow_non_contiguous=True)
        else:
            ins = self.E[q].dma_start(out=out, in_=in_)
        self.cnt[semkey] += 16
        ins.then_inc(self.sems[semkey], 16)
        self._done(semkey, self.cnt[semkey], reads, writes)

    def barrier(self):
        for e in ("pe", "act", "dve", "pool", "sp"):
            d = {s: c for s, c in self.cnt.items() if c > 0 and s != e}
            self._need(e, d)
        self.lw, self.rd = {}, {}


def build_nc(nlayers=NL, dbg=False):
    nc = bass.Bass("TRN2", target_bir_lowering=False)
    dbgT = nc.dram_tensor("dbgT", [D, 48], F32, kind="ExternalOutput").ap() if dbg else None
    dbgA = nc.dram_tensor("dbgA", [D, 48], F32, kind="ExternalOutput").ap() if dbg else None

    def din(name, shape, dt=F32):
        return nc.dram_tensor(name, shape, dt, kind="ExternalInput").ap()

    def dscr(name, shape, dt):
        return nc.dram_tensor(name, shape, dt, kind="Internal").ap()

    xT = din("xT", [D, NREAL])
    metaT = din("metaT", [D, 16])
    Wd = {n: din(n, shp) for n, shp in WSHAPES}
    vecs_d = din("vecs", [128, NV])
    mask_d = din("mask", [128, 67])
    cos_d = din("cosT", [128, NT])
    sin_d = din("sinT", [128, NT])
    perm_d = din("perm", [128, 128])
    ones_d = din("onesm", [128, 128])
    bones_d = din("bones", [128, 128])
    zeros_d = din("zeros", [128, 8])
    yT = nc.dram_tensor("yT", [D, NREAL], F32, kind="ExternalOutput").ap()
    hT = dscr("hT", [D, NT], F32)
    qT = dscr("qT", [D, NT], BF16)
    aT = dscr("aT", [D, NT], BF16)
    yS = dscr("yS", [2, D, NTY], BF16)
    cbT = dscr("cbT", [D, NT], BF16)
    sgaT = dscr("sgaT", [D, NT], BF16)
    sgcT = dscr("sgcT", [D, NT], BF16)

    def fm(ap):
        return ap.rearrange("(k p) n -> p k n", p=128)

    groups = []
    for g in range(6):
        tiles = [(0, 512), (512, 512)]
        if g == 5:
            tiles.append((1024, 48))
        groups.append(dict(g=g, G0=g * 1024, tiles=tiles, W=sum(w for _, w in tiles)))

    with ExitStack() as es:
        def sb(name, shape, dt):
            return es.enter_context(nc.sbuf_tensor(name, shape, dt))

        KT = sb("KT", [128, 2, NT], BF16)
        VX = sb("VX", [128, 50, 384], BF16)
        h = sb("h", [128, 8, 1072], F32)
        xn = sb("xn", [128, 8, 1072], BF16)
        actb = sb("actb", [128, 6, 1072], BF16)
        ring = sb("ring", [128, NRING, 4096], BF16)
        CS = sb("CS", [128, 2, 1072], F32)
        tmpf = sb("tmpf", [128, 10, 512], F32)
        stg = sb("stg", [128, 12, 514], BF16)
        vec = sb("vec", [128, NV], F32)
        msk = sb("msk", [128, 67], F32)
        perm = sb("perm_s", [128, 128], F32)
        ones = sb("ones_s", [128, 128], BF16)
        bones = sb("bones_s", [128, 128], BF16)
        hB = sb("hB", [128, 8], BF16)
        Cded = sb("Cded", [128, 8, 512], BF16)
        zt = sb("zt", [128, 8], BF16)
        ps = es.enter_context(nc.psum_tensor("ps", [128, 8, 512], F32))
        tr = Tracker(nc, es)
        st_ = {"bk": 0, "st": 0, "ring": 0, "sg": 0}

        def bk():
            b = st_["bk"]
            st_["bk"] = (b + 1) % 8
            return b

        def bk2():
            if st_["bk"] % 2:
                st_["bk"] = (st_["bk"] + 1) % 8
            b = st_["bk"]
            st_["bk"] = (b + 2) % 8
            return b

        def stn():
            s = st_["st"]
            st_["st"] = (s + 1) % 12
            return s

        def sgn():
            s = st_["sg"]
            st_["sg"] = (s + 1) % 3
            return 4 + s

        def ringv(s, nk, bw):
            return ring[:, s, 0:nk * bw].rearrange("p (k m) -> p k m", m=bw)

        def wload(src, nk, bw):
            s = st_["ring"]
            st_["ring"] = (s + 1) % NRING
            tr.dma("pool", ("ring", s), [], [("ring", s)], out=ringv(s, nk, bw), in_=src)
            return s

        mm = nc.tensor.matmul

        def pe_group(wfn, nk, sfn, tiles, banks, reads, split=False):
            if split and len(tiles) > 1:
                rest = [r for r in reads if not (isinstance(r, tuple) and r[0] == "xn")]
                for i, (c0, w) in enumerate(tiles):
                    pe_group(wfn, nk, sfn, [(c0, w)], [banks[i]], [xk(k, c0) for k in range(8)] + rest)
                return

            def fn():
                last = None
                for k in range(nk):
                    for i, (c0, w) in enumerate(tiles):
                        last = mm(ps[:, banks[i], 0:w], lhsT=wfn(k), rhs=sfn(k, c0, w), start=(k == 0), stop=(k == nk - 1))
                return last
            tr.op("pe", reads, [("ps", b) for b in banks], fn)

        tr.dma("sp", "c0", [], ["vec"], out=vec[:], in_=vecs_d)
        tr.dma("sp", "c0", [], ["msk"], out=msk[:], in_=mask_d)
        tr.dma("sp", "c0", [], ["perm"], out=perm[:], in_=perm_d)
        tr.dma("pool", "c1", [], ["ones"], out=ones[:], in_=ones_d)
        tr.dma("pool", "c1", [], ["bones"], out=bones[:], in_=bones_d)
        tr.dma("pool", "c1", [], ["zt"], out=zt[:], in_=zeros_d)
        tr.op("dve", [], ["VX"], lambda: nc.vector.memset(VX[:], 1.0))
        tr.barrier()
        for par in range(2):
            for gc in (0, 4113, 4114, 4131, 4132, 6197):
                tr.dma("sp", "c2", ["zt"], [("yS", "guard", par, gc)], out=fm(yS[par])[:, :, gc:gc + 1], in_=zt[:].rearrange("p (k o) -> p k o", o=1), slow=True)
        tr.barrier()

        xn_keys = [("xn", k, hf) for k in range(8) for hf in range(2)]

        def xk(k, c0):
            return ("xn", k, 0 if c0 < 512 else 1)
        h_keys = [("h", k) for k in range(8)]

        def norm(g, gvbase, dst_xn=True):
            for (c0, w) in g["tiles"]:
                b = bk()
                for k in range(8):
                    s = stn()
                    tr.op("act", [("h", k)], [("st", s)], lambda: nc.scalar.activation(out=stg[:, s, 0:w], in_=h[:, k, c0:c0 + w], func=AF.Square))
                    tr.op("pe", [("st", s), "ones"], [("ps", b)], lambda: mm(ps[:, b, 0:w], lhsT=ones[:], rhs=stg[:, s, 0:w], start=(k == 0), stop=(k == 7)))
                tr.op("act", [("ps", b), "vec"], [("tf", 0)], lambda: nc.scalar.activation(out=tmpf[:, 0, 0:w], in_=ps[:, b, 0:w], func=AF.Ln, bias=vec[:, EPS_COL:EPS_COL + 1], scale=1.0 / D))
                tr.op("act", [("tf", 0)], [("tf", 0)], lambda: nc.scalar.activation(out=tmpf[:, 0, 0:w], in_=tmpf[:, 0, 0:w], func=AF.Exp, scale=-0.5))
                for k in range(8):
                    if dst_xn:
                        tr.op("dve", [("h", k), ("tf", 0), "vec"], [xk(k, c0)], lambda: nc.vector.scalar_tensor_tensor(
                            out=xn[:, k, c0:c0 + w], in0=h[:, k, c0:c0 + w], scalar=vec[:, gvbase + k:gvbase + k + 1], in1=tmpf[:, 0, 0:w], op0=ALU.mult, op1=ALU.mult))
                    else:
                        tr.op("dve", [("h", k), ("tf", 0), "vec"], [("h", k)], lambda: nc.vector.scalar_tensor_tensor(
                            out=h[:, k, c0:c0 + w], in0=h[:, k, c0:c0 + w], scalar=vec[:, gvbase + k:gvbase + k + 1], in1=tmpf[:, 0, 0:w], op0=ALU.mult, op1=ALU.mult))

        def ffn(g, l, wg, wu, wdn, drain_jobs=False):
            tiles = g["tiles"]
            wgf = Wd[wg][l].rearrange("(k p) m -> p k m", p=128)
            wuf = Wd[wu][l].rearrange("(k p) m -> p k m", p=128)
            wdf = Wd[wdn][l].rearrange("(f p) m -> p f m", p=128)
            for (f0, nf) in QCH:
                for (b0, nb) in ((0, 3), (3, nf - 3)):
                    bw = nb * 128
                    cs = (f0 + b0) * 128
                    s_g = wload(wgf[:, :, cs:cs + bw], 8, bw)
                    s_u = wload(wuf[:, :, cs:cs + bw], 8, bw)
                    for j in range(nb):
                        fs = b0 + j
                        gb = [bk() for _ in tiles]
                        ub = [bk() for _ in tiles]
                        first = (f0 == 0 and b0 == 0 and j == 0)
                        pe_group(lambda k: ringv(s_g, 8, bw)[:, k, j * 128:(j + 1) * 128], 8, lambda k, c0, w: xn[:, k, c0:c0 + w], tiles, gb, xn_keys + [("ring", s_g)], split=first)
                        pe_group(lambda k: ringv(s_u, 8, bw)[:, k, j * 128:(j + 1) * 128], 8, lambda k, c0, w: xn[:, k, c0:c0 + w], tiles, ub, xn_keys + [("ring", s_u)], split=first)
                        for i, (c0, w) in enumerate(tiles):
                            t = sgn()
                            tr.op("act", [("ps", gb[i])], [("tf", t)], lambda: nc.scalar.activation(out=tmpf[:, t, 0:w], in_=ps[:, gb[i], 0:w], func=AF.Silu))
                            tr.op("dve", [("ps", ub[i]), ("tf", t)], [("act", fs)], lambda: nc.vector.tensor_tensor(out=actb[:, fs, c0:c0 + w], in0=ps[:, ub[i], 0:w], in1=tmpf[:, t, 0:w], op=ALU.mult))
                        if drain_jobs:
                            drain(1)
                for mb in range(2):
                    s_d = wload(wdf[:, f0:f0 + nf, mb * 512:(mb + 1) * 512], nf, 512)
                    for j in range(4):
                        m = mb * 4 + j
                        ob = [bk() for _ in tiles]
                        pe_group(lambda fc: ringv(s_d, nf, 512)[:, fc, j * 128:(j + 1) * 128], nf, lambda fc, c0, w: actb[:, fc, c0:c0 + w], tiles, ob,
                                 [("act", fc) for fc in range(nf)] + [("ring", s_d)])
                        for i, (c0, w) in enumerate(tiles):
                            tr.op("dve", [("ps", ob[i]), ("h", m)], [("h", m)], lambda: nc.vector.scalar_tensor_tensor(
                                out=h[:, m, c0:c0 + w], in0=ps[:, ob[i], 0:w], scalar=0.5, in1=h[:, m, c0:c0 + w], op0=ALU.mult, op1=ALU.add))

        def store_cols(dst_fm, key, chunk, s, g, c0, w):
            col = g["G0"] + c0
            tr.dma("sp", ("st", s), [("st", s)], [(key, chunk, col)], out=dst_fm[:, chunk, col:col + w], in_=stg[:, s, 0:w])

        def qk_chunk(g, l, bs, tiles, gcol, is_q, chunk):
            n = len(tiles)
            T = [(1 + 3 * i, 2 + 3 * i, 3 + 3 * i) for i in range(n)]
            sq = []
            for i, (c0, w) in enumerate(tiles):
                t1 = T[i][0]
                b = bs[i]
                tr.op("act", [("ps", b)], [("tf", t1)], lambda: nc.scalar.activation(out=tmpf[:, t1, 0:w], in_=ps[:, b, 0:w], func=AF.Copy))
                s_ = stn()
                sq.append(s_)
                tr.op("dve", [("tf", t1)], [("st", s_)], lambda: nc.vector.tensor_tensor(out=stg[:, s_, 0:w], in0=tmpf[:, t1, 0:w], in1=tmpf[:, t1, 0:w], op=ALU.mult))
            yield
            b2s = []
            for i, (c0, w) in enumerate(tiles):
                b2 = bk()
                b2s.append(b2)
                s_ = sq[i]
                tr.op("pe", [("st", s_), "bones"], [("ps", b2)], lambda: mm(ps[:, b2, 0:w], lhsT=bones[:], rhs=stg[:, s_, 0:w], start=True, stop=True))
            for i, (c0, w) in enumerate(tiles):
                t1, t2, t3 = T[i]
                b2 = b2s[i]
                tr.op("act", [("ps", b2), "vec"], [("tf", t2)], lambda: nc.scalar.activation(out=tmpf[:, t2, 0:w], in_=ps[:, b2, 0:w], func=AF.Ln, bias=vec[:, EPS_COL:EPS_COL + 1], scale=1.0 / 64))
                tr.op("act", [("tf", t2)], [("tf", t2)], lambda: nc.scalar.activation(out=tmpf[:, t2, 0:w], in_=tmpf[:, t2, 0:w], func=AF.Exp, scale=-0.5))
                tr.op("dve", [("tf", t1), ("tf", t2), "vec"], [("tf", t1)], lambda: nc.vector.scalar_tensor_tensor(
                    out=tmpf[:, t1, 0:w], in0=tmpf[:, t1, 0:w], scalar=vec[:, gcol:gcol + 1], in1=tmpf[:, t2, 0:w], op0=ALU.mult, op1=ALU.mult))
            yield
            b3s = []
            for i, (c0, w) in enumerate(tiles):
                t1 = T[i][0]
                b3 = bk()
                b3s.append(b3)
                tr.op("pe", [("tf", t1), "perm"], [("ps", b3)], lambda: mm(ps[:, b3, 0:w], lhsT=perm[:], rhs=tmpf[:, t1, 0:w], start=True, stop=True))
            for i, (c0, w) in enumerate(tiles):
                t1, t2, t3 = T[i]
                b3 = b3s[i]
                tr.op("dve", [("tf", t1), "CS"], [("tf", t3)], lambda: nc.vector.tensor_tensor(out=tmpf[:, t3, 0:w], in0=tmpf[:, t1, 0:w], in1=CS[:, 0, c0:c0 + w], op=ALU.mult))
                tr.op("dve", [("ps", b3), "CS"], [("tf", t2)], lambda: nc.vector.tensor_tensor(out=tmpf[:, t2, 0:w], in0=ps[:, b3, 0:w], in1=CS[:, 1, c0:c0 + w], op=ALU.mult))
                if is_q:
                    s2 = stn()
                    tr.op("dve", [("tf", t2), ("tf", t3)], [("st", s2)], lambda: nc.vector.tensor_tensor(out=stg[:, s2, 0:w], in0=tmpf[:, t3, 0:w], in1=tmpf[:, t2, 0:w], op=ALU.add))
                    store_cols(fm(qT), "qT", chunk, s2, g, c0, w)
                else:
                    col = g["G0"] + c0
                    tr.op("dve", [("tf", t2), ("tf", t3)], [("KT", chunk)], lambda: nc.vector.tensor_tensor(out=KT[:, chunk, col:col + w], in0=tmpf[:, t3, 0:w], in1=tmpf[:, t2, 0:w], op=ALU.add))
            yield

        def w_in_pass(g, l):
            tiles = g["tiles"]
            G0, Wg = g["G0"], g["W"]
            win = Wd["w_in"][l]
            ysf = fm(yS[l % 2])
            winf = win.rearrange("(k p) m -> p k m", p=128)
            vb = l * VPL
            tr.dma("sp", "cs", [], ["CS"], out=CS[:, 0, 0:Wg], in_=cos_d[:, G0:G0 + Wg])
            tr.dma("sp", "cs", [], ["CS"], out=CS[:, 1, 0:Wg], in_=sin_d[:, G0:G0 + Wg])

            nmain = [0]

            def main(s, j):
                bs = [bk() for _ in tiles]
                pe_group(lambda k: ringv(s, 8, 512)[:, k, j * 128:(j + 1) * 128], 8, lambda k, c0, w: xn[:, k, c0:c0 + w], tiles, bs, xn_keys + [("ring", s)], split=(nmain[0] == 0))
                nmain[0] += 1
                return bs

            def load_q(qb):
                s = st_["ring"]
                st_["ring"] = (s + 1) % NRING
                sv = ringv(s, 8, 512).rearrange("p k (c hf j) -> p k c hf j", c=4, hf=2, j=64)
                src = winf[:, :, qb * 512:(qb + 1) * 512].rearrange("p k (hf c j) -> p k c hf j", hf=2, c=4, j=64)
                for hf in range(2):
                    for cl in range(4):
                        tr.dma("pool", ("ring", s), [], [("ring", s)], out=sv[:, :, cl, hf, :], in_=src[:, :, cl, hf, :])
                return s

            def evac_simple(bs, func, dstT, key, chunk):
                for i, (c0, w) in enumerate(tiles):
                    s2 = stn()
                    tr.op("act", [("ps", bs[i])], [("st", s2)], lambda: nc.scalar.activation(out=stg[:, s2, 0:w], in_=ps[:, bs[i], 0:w], func=func))
                    store_cols(fm(dstT), key, chunk, s2, g, c0, w)

            def cc_copy(bc):
                for i, (c0, w) in enumerate(tiles):
                    t = 3 + 3 * i
                    tr.op("act", [("ps", bc[i])], [("tf", t)], lambda: nc.scalar.activation(out=tmpf[:, t, 0:w], in_=ps[:, bc[i], 0:w], func=AF.Copy))

            def cx_mult(bx, chunk):
                for i, (c0, w) in enumerate(tiles):
                    t = 3 + 3 * i
                    s2 = stn()
                    tr.op("dve", [("ps", bx[i]), ("tf", t)], [("st", s2)], lambda: nc.vector.tensor_tensor(out=stg[:, s2, 0:w], in0=ps[:, bx[i], 0:w], in1=tmpf[:, t, 0:w], op=ALU.mult))
                    col = G0 + c0
                    if w == 48:
                        for sgm in range(3):
                            yc = ycol(col + sgm * 16)
                            tr.dma("sp", ("st", s2), [("st", s2)], [("yS", chunk, "m", sgm)], out=ysf[:, chunk, yc:yc + 16], in_=stg[:, s2, sgm * 16:(sgm + 1) * 16])
                    else:
                        yc = ycol(col)
                        tr.dma("sp", ("st", s2), [("st", s2)], [("yS", chunk, col)], out=ysf[:, chunk, yc:yc + w], in_=stg[:, s2, 0:w])
                        if col == 2048:
                            tr.dma("sp", ("st", s2), [("st", s2)], [("yS", chunk, "b15r")], out=ysf[:, chunk, 4131:4132], in_=stg[:, s2, 0:1], slow=True)

            s_q = load_q(0)
            s_cb0 = wload(winf[:, :, 1536:1536 + 512], 8, 512)
            s_cb1 = wload(winf[:, :, 1536 + 512:1536 + 1024], 8, 512)
            for j in range(4):
                bs = main(s_q, j)
                st1 = qk_chunk(g, l, bs, tiles, vb + 56, True, j)
                next(st1)
                b1 = main(s_cb0, j)
                evac_simple(b1, AF.Copy, cbT, "cbT", j)
                next(st1)
                b2 = main(s_cb1, j)
                evac_simple(b2, AF.Copy, cbT, "cbT", 4 + j)
                next(st1)
            s_q = load_q(1)
            s_c = wload(winf[:, :, 2560:2560 + 512], 8, 512)
            s_x = wload(winf[:, :, 3584:3584 + 512], 8, 512)
            for j in range(4):
                bs = main(s_q, j)
                st1 = qk_chunk(g, l, bs, tiles, vb + 56, True, 4 + j)
                next(st1)
                bc = main(s_c, j)
                cc_copy(bc)
                next(st1)
                bx = main(s_x, j)
                cx_mult(bx, j)
                next(st1)
            s = wload(winf[:, :, 1024:1536], 8, 512)
            s_c = wload(winf[:, :, 2560 + 512:2560 + 1024], 8, 512)
            s_x = wload(winf[:, :, 3584 + 512:3584 + 1024], 8, 512)
            for j in range(2):
                bs = main(s, j)
                st1 = qk_chunk(g, l, bs, tiles, vb + 57, False, j)
                next(st1)
                bc = main(s_c, 2 * j)
                cc_copy(bc)
                next(st1)
                bx = main(s_x, 2 * j)
                cx_mult(bx, 4 + 2 * j)
                next(st1)
                bc = main(s_c, 2 * j + 1)
                cc_copy(bc)
                bx = main(s_x, 2 * j + 1)
                cx_mult(bx, 4 + 2 * j + 1)
            vblocks = [(j * 128, 128, (G0 + j * 128) // 128) for j in range(8)]
            if g["g"] == 5:
                vblocks += [(1024, 32, 48), (1056, 16, 49)]
            for (c0, nr, kc) in vblocks:
                b = bk()

                def fnv():
                    last = None
                    for k in range(8):
                        last = mm(ps[0:nr, b, 0:256], lhsT=xn[:, k, c0:c0 + nr], rhs=ringv(s, 8, 512)[:, k, 256:512], start=(k == 0), stop=(k == 7))
                    return last
                tr.op("pe", xn_keys + [("ring", s)], [("ps", b)], fnv)

                def fnc():
                    last = None
                    for hd in range(4):
                        dst = (hd // 2) * 192 + (hd % 2) * 128
                        last = nc.vector.tensor_copy(out=VX[0:nr, kc, dst:dst + 64], in_=ps[0:nr, b, hd * 64:(hd + 1) * 64])
                    return last
                tr.op("dve", [("ps", b)], [("VX", kc)], fnc)
            for (base, dstT, key) in ((4608, sgaT, "sgaT"), (5632, sgcT, "sgcT")):
                for blk in range(2):
                    s_g = wload(winf[:, :, base + blk * 512:base + (blk + 1) * 512], 8, 512)
                    for j in range(4):
                        b1 = main(s_g, j)
                        evac_simple(b1, AF.Sigmoid, dstT, key, blk * 4 + j)

        A_t = xn[:, :, 0:512]
        C_xn = xn[:, :, 536:1048]
        M_t = actb[:, :, :].rearrange("p f n -> p (f n)")[:, 0:4096].rearrange("p (k n) -> p k n", n=512)
        a_keys = [("xn", k, 0) for k in range(8)]
        m_keys = [("act", f) for f in range(6)]
        pending = []

        def cbuf(ti):
            if ti == 1:
                return C_xn, [("xn", k, 1) for k in range(8)]
            return Cded, ["Cd"]

        def conv_jobs(g, l, ti):
            G0 = g["G0"]
            vb = l * VPL
            cbf = fm(cbT)
            ysf = fm(yS[l % 2])
            (c0, w) = g["tiles"][ti]
            col = G0 + c0
            Cb, ckeys = cbuf(ti)
            if w == 48:
                segs = [(0, 16, 0), (16, 16, 4114), (32, 16, 4132)]
            else:
                segs = [(0, w, ycol(col) - 1)]

            def job(c):
                if c == 0 and col == 2048:
                    tr.dma("sp", "hb", [("yS", "any")], ["hB"], out=hB[:].rearrange("p (k o) -> p k o", o=1), in_=ysf[:, :, 4130:4131], slow=True)
                for (o0, sw, y0) in segs:
                    sy = stn()
                    tr.dma("sp", ("st", sy), [("yS", "any")], [("st", sy)], out=stg[:, sy, 0:sw + 2], in_=ysf[:, c, y0:y0 + sw + 2])
                    sc = stn()
                    tr.dma("sp", ("st", sc), [("cbT", "any")], [("st", sc)], out=stg[:, sc, 0:sw], in_=cbf[:, c, col + o0:col + o0 + sw])
                    if col == 1536:
                        tr.op("dve", [("st", sy), "vec"], [("st", sy)], lambda: nc.vector.tensor_scalar(
                            out=stg[:, sy, sw + 1:sw + 2], in0=stg[:, sy, sw + 1:sw + 2], scalar1=vec[:, FP_COL:FP_COL + 1], scalar2=None, op0=ALU.mult))
                    if col == 2048:
                        tr.op("dve", [("st", sy), "vec"], [("st", sy)], lambda: nc.vector.tensor_scalar(
                            out=stg[:, sy, 0:1], in0=stg[:, sy, 0:1], scalar1=vec[:, FP_COL:FP_COL + 1], scalar2=None, op0=ALU.mult))
                        tr.op("dve", [("st", sy), "vec", "hB"], [("st", sy)], lambda: nc.vector.scalar_tensor_tensor(
                            out=stg[:, sy, 0:1], in0=hB[:, c:c + 1], scalar=vec[:, FP_COL + 1:FP_COL + 2], in1=stg[:, sy, 0:1], op0=ALU.mult, op1=ALU.add))
                    tr.op("dve", [("st", sy), "vec"], [("tf", 3)], lambda: nc.vector.tensor_scalar(
                        out=tmpf[:, 3, 0:sw], in0=stg[:, sy, 0:sw], scalar1=vec[:, vb + 24 + c:vb + 25 + c], scalar2=None, op0=ALU.mult))
                    tr.op("dve", [("st", sy), "vec", ("tf", 3)], [("tf", 3)], lambda: nc.vector.scalar_tensor_tensor(
                        out=tmpf[:, 3, 0:sw], in0=stg[:, sy, 1:sw + 1], scalar=vec[:, vb + 32 + c:vb + 33 + c], in1=tmpf[:, 3, 0:sw], op0=ALU.mult, op1=ALU.add))
                    tr.op("dve", [("st", sy), "vec", ("tf", 3)], [("tf", 3)], lambda: nc.vector.scalar_tensor_tensor(
                        out=tmpf[:, 3, 0:sw], in0=stg[:, sy, 2:sw + 2], scalar=vec[:, vb + 40 + c:vb + 41 + c], in1=tmpf[:, 3, 0:sw], op0=ALU.mult, op1=ALU.add))
                    tr.op("dve", [("st", sc), "vec", ("tf", 3)], ckeys, lambda: nc.vector.scalar_tensor_tensor(
                        out=Cb[:, c, o0:o0 + sw], in0=tmpf[:, 3, 0:sw], scalar=vec[:, vb + 48 + c:vb + 49 + c], in1=stg[:, sc, 0:sw], op0=ALU.add, op1=ALU.mult))
            return [(lambda c=c: job(c)) for c in range(8)]

        def drain(n=None):
            k = 0
            while pending and (n is None or k < n):
                pending.pop(0)()
                k += 1

        def proj_tile(g, l, ti):
            G0 = g["G0"]
            (c0, w) = g["tiles"][ti]
            col = G0 + c0
            aTf, sgaf, sgcf = fm(aT), fm(sgaT), fm(sgcT)
            woa = Wd["w_o_attn"][l]
            wocf = Wd["w_o_conv"][l].rearrange("(k p) m -> p k m", p=128)
            wmf = Wd["w_merge"][l].rearrange("(k p) m -> p k m", p=128)
            Cb, ckeys = cbuf(ti)
            tr.dma("sp", "aload", [("aT", col)], a_keys, out=A_t[:, :, 0:w], in_=aTf[:, :, col:col + w])
            for blk in range(2):
                s_a = st_["ring"]
                st_["ring"] = (s_a + 1) % NRING
                for hf in range(2):
                    for cb_ in range(2):
                        src = woa[:, blk * 512:(blk + 1) * 512].rearrange("(cb hf cl r) m -> cb hf r cl m", cb=2, hf=2, cl=4, r=64)[cb_, hf]
                        dst = ringv(s_a, 8, 512)[hf * 64:(hf + 1) * 64, cb_ * 4:(cb_ + 1) * 4, :]
                        tr.dma("pool", ("ring", s_a), [], [("ring", s_a)], out=dst, in_=src)
                s_c = wload(wocf[:, :, blk * 512:(blk + 1) * 512], 8, 512)
                for j in range(4):
                    m = blk * 4 + j
                    ba, bc = bk(), bk()
                    pe_group(lambda k: ringv(s_a, 8, 512)[:, k, j * 128:(j + 1) * 128], 8, lambda k, cc0, ww: A_t[:, k, 0:ww], [(0, w)], [ba], a_keys + [("ring", s_a)])
                    pe_group(lambda k: ringv(s_c, 8, 512)[:, k, j * 128:(j + 1) * 128], 8, lambda k, cc0, ww: Cb[:, k, 0:ww], [(0, w)], [bc], ckeys + [("ring", s_c)])
                    sa, sc2 = stn(), stn()
                    tr.dma("sp", ("st", sa), [("sgaT", "any")], [("st", sa)], out=stg[:, sa, 0:w], in_=sgaf[:, m, col:col + w])
                    tr.dma("sp", ("st", sc2), [("sgcT", "any")], [("st", sc2)], out=stg[:, sc2, 0:w], in_=sgcf[:, m, col:col + w])
                    tr.op("dve", [("ps", ba), ("st", sa)], [("tf", 1)], lambda: nc.vector.tensor_tensor(out=tmpf[:, 1, 0:w], in0=ps[:, ba, 0:w], in1=stg[:, sa, 0:w], op=ALU.mult))
                    tr.op("dve", [("ps", bc), ("st", sc2)], [("tf", 2)], lambda: nc.vector.tensor_tensor(out=tmpf[:, 2, 0:w], in0=ps[:, bc, 0:w], in1=stg[:, sc2, 0:w], op=ALU.mult))
                    tr.op("dve", [("tf", 1), ("tf", 2)], m_keys, lambda: nc.vector.tensor_tensor(out=M_t[:, m, 0:w], in0=tmpf[:, 1, 0:w], in1=tmpf[:, 2, 0:w], op=ALU.add))
                    drain(1)
            for blk in range(2):
                s_m = wload(wmf[:, :, blk * 512:(blk + 1) * 512], 8, 512)
                for j in range(4):
                    m = blk * 4 + j
                    bo = bk()
                    pe_group(lambda k: ringv(s_m, 8, 512)[:, k, j * 128:(j + 1) * 128], 8, lambda k, cc0, ww: M_t[:, k, 0:ww], [(0, w)], [bo], m_keys + [("ring", s_m)])
                    tr.op("dve", [("ps", bo), ("h", m)], [("h", m)], lambda: nc.vector.tensor_tensor(out=h[:, m, c0:c0 + w], in0=ps[:, bo, 0:w], in1=h[:, m, c0:c0 + w], op=ALU.add))

        def p2_front(g, l, prefetched):
            nt = len(g["tiles"])
            if not prefetched:
                pending.extend(conv_jobs(g, l, 0))
            drain()
            for ti in range(nt):
                if ti + 1 < nt:
                    pending.extend(conv_jobs(g, l, ti + 1))
                proj_tile(g, l, ti)
                drain()

        def attention(l):
            qTf, aTf = fm(qT), fm(aT)
            Pb = [actb[:, i, 0:1024] for i in range(2)]
            Ao = actb[:, :, :].rearrange("p f n -> p (f n)")[:, 2144:2144 + 4096].rearrange("p (k n) -> p k n", n=512)
            Qb = [xn[:, :, 0:512], xn[:, :, 536:1048]]
            slots = []
            kch0 = [(j * 128, 128, j, None) for j in range(32)] + [(6144, 32, 48, None)]
            qt0 = [(j * 512, 512, j // 4) for j in range(8)] + [(6144, 16, 0), (6160, 16, 1)]
            kch1 = [(4096 + j * 128, 128, 32 + j, None) for j in range(16)] + [(6176, 16, 49, None)]
            qt1 = [(4096 + j * 512, 512, -1) for j in range(4)] + [(6176, 16, -1)]
            xyc = [0]
            items = [(qc, w, half, kch0) for (qc, w, half) in qt0] + [(qc, w, half, kch1) for (qc, w, half) in qt1]

            def qload(n):
                (qc_, w_, _, _) = items[n]
                tr.dma("sp", ("Q", n % 2), [("qT", "any")], [("Q", n % 2)], out=Qb[n % 2][:, :, 0:w_], in_=qTf[:, :, qc_:qc_ + w_])
            qload(0)
            for qi, (qc, w, half, kchs) in enumerate(items):
                if True:
                    Q = Qb[qi % 2]
                    qkey = ("Q", qi % 2)
                    if qi + 1 < len(items):
                        qload(qi + 1)
                    for c in range(8):
                        kchunk = c // 4
                        pair = c // 4
                        bX, bY = (4, 5) if (xyc[0] % 2 == 0) else (6, 7)
                        xyc[0] += 1
                        nk = len(kchs)
                        state = {}

                        def s_step(i):
                            (kc0, nr, vxc, _) = kchs[i]
                            b2 = 2 * (i % 2)
                            state[i] = b2

                            def fn():
                                mm(ps[0:nr, b2, 0:w], lhsT=KT[0:64, kchunk, kc0:kc0 + nr], rhs=Q[0:64, c, 0:w], start=True, stop=True)
                                return mm(ps[0:nr, b2 + 1, 0:w], lhsT=KT[64:128, kchunk, kc0:kc0 + nr], rhs=Q[64:128, c, 0:w], start=True, stop=True)
                            tr.op("pe", [qkey, ("KT", kchunk)], [("ps", b2), ("ps", b2 + 1)], fn)

                        def e_step(i):
                            (kc0, nr, vxc, _) = kchs[i]
                            b2 = state[i]
                            P = Pb[i % 2]
                            pk = ("P", i % 2)
                            mc = ZERO_COL_M if half < 0 else half * 33 + i
                            tr.op("act", [("ps", b2), ("ps", b2 + 1), "msk"], [pk], lambda: nc.scalar.activation(
                                out=P[0:nr, :].rearrange("p (t n) -> p t n", t=2)[:, :, 0:w], in_=ps[0:nr, b2:b2 + 2, 0:w], func=AF.Exp, bias=msk[0:nr, mc:mc + 1], scale=0.125))

                        def v_step(i):
                            (kc0, nr, vxc, _) = kchs[i]
                            P = Pb[i % 2]
                            pk = ("P", i % 2)

                            def fn():
                                mm(ps[:, bX, 0:w], lhsT=VX[0:nr, vxc, pair * 192:pair * 192 + 128], rhs=P[0:nr, 0:w], start=(i == 0), stop=(i == nk - 1))
                                return mm(ps[:, bY, 0:w], lhsT=VX[0:nr, vxc, pair * 192 + 64:pair * 192 + 192], rhs=P[0:nr, 512:512 + w], start=(i == 0), stop=(i == nk - 1))
                            tr.op("pe", [pk, ("VX", vxc)], [("ps", bX), ("ps", bY)], fn)

                        s_step(0)
                        if nk > 1:
                            s_step(1)
                        for i in range(nk):
                            e_step(i)
                            if i + 2 < nk:
                                s_step(i + 2)
                            v_step(i)
                        tr.op("dve", [("ps", bX)], [("tf", 1)], lambda: nc.vector.reciprocal(out=tmpf[64:128, 1, 0:w], in_=ps[64:128, bX, 0:w]))
                        tr.op("dve", [("ps", bY)], [("tf", 2)], lambda: nc.vector.reciprocal(out=tmpf[0:64, 2, 0:w], in_=ps[0:64, bY, 0:w]))
                        tr.op("dve", [("ps", bX), ("tf", 1)], [("Ao", c, 0)], lambda: nc.vector.tensor_tensor(out=Ao[0:64, c, 0:w], in0=ps[0:64, bX, 0:w], in1=tmpf[64:128, 1, 0:w], op=ALU.mult))
                        tr.op("dve", [("ps", bY), ("tf", 2)], [("Ao", c, 1)], lambda: nc.vector.tensor_tensor(out=Ao[64:128, c, 0:w], in0=ps[64:128, bY, 0:w], in1=tmpf[0:64, 2, 0:w], op=ALU.mult))
                    tr.dma("sp", "astore", [("Ao", c_, hf_) for c_ in range(8) for hf_ in range(2)], [("aT", qc)], out=aTf[:, :, qc:qc + w], in_=Ao[:, :, 0:w])

        ZERO_COL_M = 66

        hTf = fm(hT)
        xTf = fm(xT)
        mTf = fm(metaT)
        yTf = fm(yT)
        for l in range(nlayers + 1):
            for g in groups:
                G0, Wg = g["G0"], g["W"]
                if l == 0:
                    tr.dma("act", "hload", [], h_keys, out=h[:, :, 0:1024], in_=xTf[:, :, G0:G0 + 1024])
                    if g["g"] == 5:
                        for sgm in range(3):
                            tr.dma("act", "hload", [], h_keys, out=h[:, :, 1024 + sgm * 16:1040 + sgm * 16], in_=mTf)
                else:
                    tr.dma("act", "hload", [("hT", g["g"])], h_keys, out=h[:, :, 0:Wg], in_=hTf[:, :, G0:G0 + Wg])
                    p2_front(g, l - 1, prefetched=True)
                    norm(g, (l - 1) * VPL + 16)
                    if g["g"] + 1 < len(groups):
                        pending.extend(conv_jobs(groups[g["g"] + 1], l - 1, 0))
                    ffn(g, l - 1, "ffn2_w_gate", "ffn2_w_up", "ffn2_w_down", drain_jobs=True)
                    drain()
                if l < nlayers:
                    norm(g, l * VPL + 0)
                    ffn(g, l, "ffn1_w_gate", "ffn1_w_up", "ffn1_w_down")
                    tr.dma("sp", "hstore", h_keys, [("hT", g["g"])], out=hTf[:, :, G0:G0 + Wg], in_=h[:, :, 0:Wg])
                    norm(g, l * VPL + 8)
                    w_in_pass(g, l)
                else:
                    norm(g, GF_BASE, dst_xn=False)
                    tr.dma("sp", "ostore", h_keys, [("yT", g["g"])], out=yTf[:, :, G0:G0 + 1024], in_=h[:, :, 0:1024])
                    if dbg and g["g"] == 5:
                        tr.dma("sp", "ostore", h_keys, [("dbg", 0)], out=fm(dbgT), in_=h[:, :, 1024:1072])
            if l < nlayers:
                tr.barrier()
                for job in conv_jobs(groups[0], l, 0):
                    job()
                attention(l)
                tr.barrier()
        tr.barrier()
        if dbg:
            tr.dma("sp", "dbg1", [], [("xn", 0)], out=xn[:, :, 0:48], in_=fm(aT)[:, :, 6144:6192])
            tr.op("dve", [("xn", 0)], [("h", 0)], lambda: nc.vector.tensor_copy(out=h[:, :, 0:48], in_=xn[:, :, 0:48]))
            tr.dma("sp", "dbg1", [("h", 0)], [("dbgA", 0)], out=fm(dbgA), in_=h[:, :, 0:48])
            tr.barrier()
    return nc


def _host_prep(inputs):
    xp = np.asarray(inputs["x_prompt"], dtype=np.float32)
    xs = np.asarray(inputs["x_sample"], dtype=np.float32)
    meta = np.asarray(inputs["meta_tokens"], dtype=np.float32)
    metaT = np.ascontiguousarray(meta.T)
    vecs = np.zeros((128, NV), np.float32)

    def fmv(v):
        return np.asarray(v, np.float32).reshape(8, 128).T

    for l in range(NL):
        b = l * VPL
        vecs[:, b + 0:b + 8] = fmv(inputs["ffn1_norm"][l])
        vecs[:, b + 8:b + 16] = fmv(inputs["mix_norm"][l])
        vecs[:, b + 16:b + 24] = fmv(inputs["ffn2_norm"][l])
        vecs[:, b + 24:b + 32] = fmv(inputs["conv_w"][l][0])
        vecs[:, b + 32:b + 40] = fmv(inputs["conv_w"][l][1])
        vecs[:, b + 40:b + 48] = fmv(inputs["conv_w"][l][2])
        vecs[:, b + 48:b + 56] = fmv(inputs["conv_b"][l])
        vecs[:, b + 56] = np.tile(np.asarray(inputs["q_norm"][l], np.float32), 2)
        vecs[:, b + 57] = np.tile(np.asarray(inputs["k_norm"][l], np.float32), 2)
    vecs[:, GF_BASE:GF_BASE + 8] = fmv(inputs["final_norm"])
    vecs[:, EPS_COL] = 1e-6
    vecs[:, ZERO_COL] = 0.0
    perm = np.zeros((128, 128), np.float32)
    for m in range(128):
        partner = m + 16 if (m % 32) < 16 else m - 16
        perm[partner, m] = 1.0
    onesm = np.ones((128, 128), np.float32)
    bones = np.zeros((128, 128), np.float32)
    bones[:64, :64] = 1.0
    bones[64:, 64:] = 1.0
    zeros = np.zeros((128, 8), np.float32)
    inv_freq = (10000.0 ** (-np.arange(0, 32, 2, dtype=np.float32) / 32.0)).astype(np.float32)

    def rope_tables(rows, cols):
        p = np.arange(128)
        d = p % 64
        fi = d % 16
        isrow = d < 32
        neg = (d % 32) < 16
        pos = np.where(isrow[:, None], rows[None, :], cols[None, :]).astype(np.float32)
        ang = (pos * inv_freq[fi][:, None]).astype(np.float32)
        c = np.cos(ang).astype(np.float32)
        s = np.sin(ang).astype(np.float32)
        s = np.where(neg[:, None], -s, s).astype(np.float32)
        return np.ascontiguousarray(c), np.ascontiguousarray(s)

    tok4096 = np.arange(4096)
    tok2048 = np.arange(2048)
    mrow = np.full(16, -1.0, np.float32)
    mcol = np.arange(16, dtype=np.float32)
    in_maps = []
    for core in range(8):
        if core < 4:
            s0 = xp[core]
            s1 = xs[core]
            rows = np.concatenate([tok4096 // 64, tok2048 // 64, mrow, mrow, mrow]).astype(np.float32)
            cols = np.concatenate([tok4096 % 64, tok2048 % 64, mcol, mcol, mcol]).astype(np.float32)
            fp = 1.0
        else:
            j = core - 4
            s0 = np.concatenate([xs[4 + 3 * j], xs[5 + 3 * j]], axis=0)
            s1 = xs[6 + 3 * j]
            rows = np.concatenate([tok2048 // 64, tok2048 // 64, tok2048 // 64, mrow, mrow, mrow]).astype(np.float32)
            cols = np.concatenate([tok2048 % 64, tok2048 % 64, tok2048 % 64, mcol, mcol, mcol]).astype(np.float32)
            fp = 0.0
        xT = np.ascontiguousarray(np.concatenate([s0, s1], axis=0).T)
        cosT, sinT = rope_tables(rows, cols)
        v = vecs.copy()
        v[:, FP_COL] = fp
        v[:, FP_COL + 1] = 1.0 - fp
        mask = np.zeros((128, 67), np.float32)
        if core < 4:
            for half in range(2):
                mask[16:32, half * 33 + 32] = NEG
        else:
            mask[:, 16:32] = NEG
            mask[16:32, 32] = NEG
            mask[:, 33:33 + 16] = NEG
            mask[0:16, 33 + 32] = NEG
        m = {"xT": xT, "metaT": metaT, "vecs": v, "mask": mask, "cosT": cosT, "sinT": sinT,
             "perm": perm, "onesm": onesm, "bones": bones, "zeros": zeros}
        for n, _ in WSHAPES:
            m[n] = np.asarray(inputs[n], dtype=np.float32)
        in_maps.append(m)
    return in_maps


def kernel(**inputs):
    in_maps = _host_prep(inputs)
    nc = build_nc()
    res = run_bass_kernel_spmd(nc, in_maps, core_ids=list(range(8)))
    y_prompt = np.zeros((4, 4096, D), np.float32)
    y_sample = np.zeros((16, 2048, D), np.float32)
    for core in range(8):
        y = np.asarray(res.results[core]["yT"]).T
        if core < 4:
            y_prompt[core] = y[0:4096]
            y_sample[core] = y[4096:6144]
        else:
            j = core - 4
            y_sample[4 + 3 * j] = y[0:2048]
            y_sample[5 + 3 * j] = y[2048:4096]
            y_sample[6 + 3 * j] = y[4096:6144]
    return (y_prompt, y_sample)
```
